# Optimizing a Trainium2 kernel written in Bass

```python
import math
import jax
import jax.numpy as jnp
from jax import lax
import numpy as np

D_MODEL = 2048
BATCH = 4
SEQ = 2048
DEPTH = 1
DEC_BATCH = 8
DEC_SEQ = 64
PAST_LEN = 1024

CHUNK = 64
EPS = 1e-6
NEG_INF = -1e30
Q_BLOCK = 128
DA_HEADS = 8
DA_QK_DIM = 64
DA_V_DIM = 2 * DA_QK_DIM
DA_WIDTH = DA_HEADS * DA_V_DIM
ROPE_DIM = DA_QK_DIM // 4
ROPE_THETA = 500000.0
CM_GROUPS = 4
CM_GROUP_DIM = 128
CM_WIDTH = CM_GROUPS * CM_GROUP_DIM
CM_LEN = 128
MEM_TOKENS = 256
MEM_HEADS = 4
MEM_HEAD_DIM = 128
MEM_WIDTH = MEM_HEADS * MEM_HEAD_DIM
N_BRANCH = 3
MIX_WIDTH = DA_WIDTH + CM_WIDTH + MEM_WIDTH
IN_SPLITS = (DA_WIDTH, 2 * DA_WIDTH, 3 * DA_WIDTH, 3 * DA_WIDTH + CM_WIDTH, 3 * DA_WIDTH + 2 * CM_WIDTH, 3 * DA_WIDTH + 2 * CM_WIDTH + MEM_WIDTH)
IN_WIDTH = 3 * DA_WIDTH + 2 * CM_WIDTH + MEM_WIDTH + N_BRANCH * D_MODEL
PEER_HEADS = 8
PEER_NKEYS = 128
PEER_EXPERTS = PEER_NKEYS * PEER_NKEYS
PEER_QDIM = 256
PEER_HALF = PEER_QDIM // 2
PEER_TOPK = 16
PEER_TOKEN_BLOCK = 128

kernel_name = 'streaming_hybrid_diffattn_gmlp_mem_peer'


def rms_norm(x, g):
    xf = x.astype(jnp.float32)
    y = xf * lax.rsqrt(jnp.mean(xf * xf, axis=-1, keepdims=True) + EPS)
    return (y * g.astype(jnp.float32)).astype(x.dtype)


def layer_norm(x, g, b):
    xf = x.astype(jnp.float32)
    mu = jnp.mean(xf, axis=-1, keepdims=True)
    xc = xf - mu
    var = jnp.mean(xc * xc, axis=-1, keepdims=True)
    return (xc * lax.rsqrt(var + EPS) * g.astype(jnp.float32) + b.astype(jnp.float32)).astype(x.dtype)


def rope_partial(x, pos):
    half = ROPE_DIM // 2
    inv = ROPE_THETA ** (-jnp.arange(half, dtype=jnp.float32) / half)
    ang = pos.astype(jnp.float32)[:, None] * inv[None, :]
    cos = jnp.cos(ang)[:, None, None, :]
    sin = jnp.sin(ang)[:, None, None, :]
    xr = x[..., :ROPE_DIM].astype(jnp.float32)
    x1, x2 = xr[..., :half], xr[..., half:]
    rot = jnp.concatenate([x1 * cos - x2 * sin, x2 * cos + x1 * sin], axis=-1)
    return jnp.concatenate([rot.astype(x.dtype), x[..., ROPE_DIM:]], axis=-1)


def diff_attend(q, k, v, q_pos, k_pos, lam):
    s = jnp.einsum('bqhmd,bkhmd->bhmqk', q, k).astype(jnp.float32) * (DA_QK_DIM ** -0.5)
    mask = (k_pos[None, :] // CHUNK) <= (q_pos[:, None] // CHUNK)
    s = jnp.where(mask, s, NEG_INF)
    p = jax.nn.softmax(s, axis=-1)
    a = p[:, :, 0] - lam * p[:, :, 1]
    return jnp.einsum('bhqk,bkhd->bqhd', a.astype(v.dtype), v)


def diff_attention_blocked(q, k, v, pos, lam):
    B, T = q.shape[0], q.shape[1]
    nb = T // Q_BLOCK
    qb = jnp.moveaxis(q.reshape(B, nb, Q_BLOCK, DA_HEADS, 2, DA_QK_DIM), 1, 0)
    pb = pos.reshape(nb, Q_BLOCK)
    ob = lax.map(lambda a: diff_attend(a[0], k, v, a[1], pos, lam), (qb, pb))
    return jnp.moveaxis(ob, 0, 1).reshape(B, T, DA_HEADS, DA_V_DIM)


def chunk_mlp(u, v, w_s, b_s):
    B, T, _ = u.shape
    L = min(T, CM_LEN)
    nc = T // L
    ws = jnp.tril(w_s[:, :L, :L])
    vb = v.reshape(B, nc, L, CM_GROUPS, CM_GROUP_DIM)
    s = jnp.einsum('gts,bnsgc->bntgc', ws, vb) + jnp.transpose(b_s[:, :L])[None, None, :, :, None]
    return u * s.reshape(B, T, CM_WIDTH)


def mem_kv(mem, g, w, kn_g):
    B, M, _ = mem.shape
    kv = rms_norm(mem, g) @ w
    k, v = jnp.split(kv, 2, axis=-1)
    k = rms_norm(k.reshape(B, M, MEM_HEADS, MEM_HEAD_DIM), kn_g)
    return k, v.reshape(B, M, MEM_HEADS, MEM_HEAD_DIM)


def mem_attend(q, mem_k, mem_v):
    s = jnp.einsum('bqhd,bkhd->bhqk', q, mem_k).astype(jnp.float32) * (MEM_HEAD_DIM ** -0.5)
    p = jax.nn.softmax(s, axis=-1).astype(mem_v.dtype)
    return jnp.einsum('bhqk,bkhd->bqhd', p, mem_v)


def _peer_block(xb, w_query, sub_keys, expert_u, expert_v):
    tb = xb.shape[0]
    q = (xb @ w_query).reshape(tb, PEER_HEADS, 2, PEER_HALF)
    s = jnp.einsum('thpc,hpkc->thpk', q, sub_keys).astype(jnp.float32)
    s1, i1 = lax.top_k(s[:, :, 0], PEER_TOPK)
    s2, i2 = lax.top_k(s[:, :, 1], PEER_TOPK)
    n_cand = PEER_TOPK * PEER_TOPK
    cand_s = (s1[..., :, None] + s2[..., None, :]).reshape(tb, PEER_HEADS, n_cand)
    cand_i = (i1[..., :, None] * PEER_NKEYS + i2[..., None, :]).reshape(tb, PEER_HEADS, n_cand)
    top_s, top_j = lax.top_k(cand_s, PEER_TOPK)
    eidx = jnp.take_along_axis(cand_i, top_j, axis=-1)
    g = jax.nn.softmax(top_s, axis=-1).astype(xb.dtype)
    a = jax.nn.gelu(jnp.einsum('td,thkd->thk', xb, expert_u[eidx]), approximate=False)
    return jnp.einsum('thk,thkd->td', g * a, expert_v[eidx])


def peer(x, w_query, sub_keys, expert_u, expert_v):
    B, T, D = x.shape
    n = B * T
    pad = (-n) % PEER_TOKEN_BLOCK
    xf = jnp.pad(x.reshape(n, D), ((0, pad), (0, 0)))
    out = lax.map(lambda xb: _peer_block(xb, w_query, sub_keys, expert_u, expert_v), xf.reshape(-1, PEER_TOKEN_BLOCK, D))
    return out.reshape(-1, D)[:n].reshape(B, T, D)


def setup_inputs(seed: int = 0) -> dict:
    key = jax.random.key(seed)
    ks = jax.random.split(key, 32)
    f32 = jnp.float32

    def nrm(k, shape, scale):
        return jax.random.normal(k, shape, f32) * scale

    def gain(k, shape):
        return 1.0 + 0.01 * jax.random.normal(k, shape, f32)

    L = DEPTH
    return {
        'x_prompt': nrm(ks[0], (BATCH, SEQ, D_MODEL), 1.0),
        'x_sample': nrm(ks[1], (DEC_BATCH, DEC_SEQ, D_MODEL), 1.0),
        'mem_prompt': nrm(ks[2], (BATCH, MEM_TOKENS, D_MODEL), 1.0),
        'cache_da_k': nrm(ks[3], (L, DEC_BATCH, PAST_LEN, DA_HEADS, 2 * DA_QK_DIM), 1.0),
        'cache_da_v': nrm(ks[4], (L, DEC_BATCH, PAST_LEN, DA_HEADS, DA_V_DIM), 1.0),
        'cache_mem_k': nrm(ks[5], (L, DEC_BATCH, MEM_TOKENS, MEM_HEADS, MEM_HEAD_DIM), 1.0),
        'cache_mem_v': nrm(ks[6], (L, DEC_BATCH, MEM_TOKENS, MEM_HEADS, MEM_HEAD_DIM), 1.0),
        'norm_mix_g': gain(ks[7], (L, D_MODEL)),
        'w_in': nrm(ks[8], (L, D_MODEL, IN_WIDTH), D_MODEL ** -0.5),
        'b_gate': nrm(ks[9], (L, N_BRANCH * D_MODEL), 0.01),
        'da_qn_g': gain(ks[10], (L, DA_QK_DIM)),
        'da_kn_g': gain(ks[11], (L, DA_QK_DIM)),
        'da_lambda_q1': nrm(ks[12], (L, DA_QK_DIM), 0.1),
        'da_lambda_k1': nrm(ks[13], (L, DA_QK_DIM), 0.1),
        'da_lambda_q2': nrm(ks[14], (L, DA_QK_DIM), 0.1),
        'da_lambda_k2': nrm(ks[15], (L, DA_QK_DIM), 0.1),
        'da_out_g': gain(ks[16], (L, DA_V_DIM)),
        'cm_ln_g': gain(ks[17], (L, CM_WIDTH)),
        'cm_ln_b': nrm(ks[18], (L, CM_WIDTH), 0.01),
        'cm_ws': nrm(ks[19], (L, CM_GROUPS, CM_LEN, CM_LEN), CM_LEN ** -0.5),
        'cm_bs': gain(ks[20], (L, CM_GROUPS, CM_LEN)),
        'norm_mem_g': gain(ks[21], (L, D_MODEL)),
        'w_mem_kv': nrm(ks[22], (L, D_MODEL, 2 * MEM_WIDTH), D_MODEL ** -0.5),
        'mem_qn_g': gain(ks[23], (L, MEM_HEAD_DIM)),
        'mem_kn_g': gain(ks[24], (L, MEM_HEAD_DIM)),
        'w_branch': nrm(ks[25], (L, MIX_WIDTH, D_MODEL), MIX_WIDTH ** -0.5),
        'w_out': nrm(ks[26], (L, D_MODEL, D_MODEL), D_MODEL ** -0.5),
        'norm_ffn_g': gain(ks[27], (L, D_MODEL)),
        'peer_w_query': nrm(ks[28], (L, D_MODEL, PEER_HEADS * PEER_QDIM), D_MODEL ** -0.5),
        'peer_sub_keys': nrm(ks[29], (L, PEER_HEADS, 2, PEER_NKEYS, PEER_HALF), PEER_HALF ** -0.5),
        'peer_u': nrm(ks[30], (L, PEER_EXPERTS, D_MODEL), D_MODEL ** -0.5),
        'peer_v': nrm(ks[31], (L, PEER_EXPERTS, D_MODEL), (PEER_HEADS * PEER_TOPK) ** -0.5),
    }


def reference(x_prompt, x_sample, mem_prompt, cache_da_k, cache_da_v, cache_mem_k, cache_mem_v,
              norm_mix_g, w_in, b_gate, da_qn_g, da_kn_g, da_lambda_q1, da_lambda_k1,
              da_lambda_q2, da_lambda_k2, da_out_g, cm_ln_g, cm_ln_b, cm_ws, cm_bs,
              norm_mem_g, w_mem_kv, mem_qn_g, mem_kn_g, w_branch, w_out, norm_ffn_g,
              peer_w_query, peer_sub_keys, peer_u, peer_v):
    f32 = jnp.float32

    def mix_layer(l, x, pos, past_k, past_v, mem_k, mem_v):
        B, T, _ = x.shape
        lam_init = 0.8 - 0.6 * math.exp(-0.3 * l)
        lam = (jnp.exp(jnp.sum(da_lambda_q1[l].astype(f32) * da_lambda_k1[l].astype(f32)))
               - jnp.exp(jnp.sum(da_lambda_q2[l].astype(f32) * da_lambda_k2[l].astype(f32))) + lam_init)
        h = rms_norm(x, norm_mix_g[l])
        p = h @ w_in[l]
        q_da, k_da, v_da, u_cm, v_cm, q_mem, g_logit = jnp.split(p, list(IN_SPLITS), axis=-1)
        q = rope_partial(rms_norm(q_da.reshape(B, T, DA_HEADS, 2, DA_QK_DIM), da_qn_g[l]), pos)
        k = rope_partial(rms_norm(k_da.reshape(B, T, DA_HEADS, 2, DA_QK_DIM), da_kn_g[l]), pos)
        v = v_da.reshape(B, T, DA_HEADS, DA_V_DIM)
        if past_k is None:
            o = diff_attention_blocked(q, k, v, pos, lam)
        else:
            P = past_k.shape[1]
            k_all = jnp.concatenate([past_k.reshape(B, P, DA_HEADS, 2, DA_QK_DIM), k], axis=1)
            v_all = jnp.concatenate([past_v, v], axis=1)
            o = diff_attend(q, k_all, v_all, pos, jnp.arange(P + T), lam)
        o_da = (rms_norm(o, da_out_g[l]) * (1.0 - lam_init)).reshape(B, T, DA_WIDTH)
        u = jax.nn.gelu(u_cm, approximate=False)
        vn = layer_norm(jax.nn.gelu(v_cm, approximate=False), cm_ln_g[l], cm_ln_b[l])
        o_cm = chunk_mlp(u, vn, cm_ws[l], cm_bs[l])
        qm = rms_norm(q_mem.reshape(B, T, MEM_HEADS, MEM_HEAD_DIM), mem_qn_g[l])
        o_mem = mem_attend(qm, mem_k, mem_v).reshape(B, T, MEM_WIDTH)
        wb = w_branch[l]
        gates = jax.nn.sigmoid(g_logit.reshape(B, T, N_BRANCH, D_MODEL) + b_gate[l].reshape(N_BRANCH, D_MODEL))
        merged = (gates[:, :, 0] * (o_da @ wb[:DA_WIDTH])
                  + gates[:, :, 1] * (o_cm @ wb[DA_WIDTH:DA_WIDTH + CM_WIDTH])
                  + gates[:, :, 2] * (o_mem @ wb[DA_WIDTH + CM_WIDTH:]))
        x = x + merged @ w_out[l]
        x = x + peer(rms_norm(x, norm_ffn_g[l]), peer_w_query[l], peer_sub_keys[l], peer_u[l], peer_v[l])
        return x, k.reshape(B, T, DA_HEADS, 2 * DA_QK_DIM), v, vn

    pos_p = jnp.arange(x_prompt.shape[1])
    pos_s = cache_da_k.shape[2] + jnp.arange(x_sample.shape[1])
    yp, ys = x_prompt, x_sample
    kp_l, vp_l, mkp_l, mvp_l, ks_l, vs_l, cvs_l = [], [], [], [], [], [], []
    for l in range(DEPTH):
        mk, mv = mem_kv(mem_prompt, norm_mem_g[l], w_mem_kv[l], mem_kn_g[l])
        yp, kp, vp, _ = mix_layer(l, yp, pos_p, None, None, mk, mv)
        ys, ks, vs, cvs = mix_layer(l, ys, pos_s, cache_da_k[l], cache_da_v[l], cache_mem_k[l], cache_mem_v[l])
        kp_l.append(kp)
        vp_l.append(vp)
        mkp_l.append(mk)
        mvp_l.append(mv)
        ks_l.append(ks)
        vs_l.append(vs)
        cvs_l.append(cvs)
    new_da_k_prompt = jnp.stack(kp_l)
    new_da_v_prompt = jnp.stack(vp_l)
    new_mem_k_prompt = jnp.stack(mkp_l)
    new_mem_v_prompt = jnp.stack(mvp_l)
    new_da_k_sample = jnp.stack(ks_l)
    new_da_v_sample = jnp.stack(vs_l)
    new_cm_v_sample = jnp.stack(cvs_l)
    return (yp, ys, new_da_k_prompt, new_da_v_prompt, new_mem_k_prompt, new_mem_v_prompt, new_da_k_sample, new_da_v_sample, new_cm_v_sample)
```

```python
from contextlib import ExitStack
import numpy as np
import concourse.bass as bass
import concourse.mybir as mybir
from concourse.bass_utils import run_bass_kernel_spmd

F32 = mybir.dt.float32
BF16 = mybir.dt.bfloat16
I32 = mybir.dt.int32
U32 = mybir.dt.uint32
ALU = mybir.AluOpType
AF = mybir.ActivationFunctionType
AX = mybir.AxisListType

DEBUG = False
POOL_TT = "vector"
EPS = 1e-6
NA, NS, NB = 1024, 64, 1024
OWN = NA + NS
OWN_BLKS = {0: [0, 3, 4, 7, 8, 11, 12, 15], 1: [1, 2, 5, 6, 9, 10, 13, 14]}
GQ, GK, GOUT, LNG, LNB, MQG, MKG, LQ = 0, 64, 128, 256, 768, 1280, 1408, 1536
NSV = 1792


class _Stop(Exception):
    pass


class _RotBuf:
    def __init__(self, bufs):
        self.bufs = bufs
        self.i = 0

    def __getitem__(self, k):
        return self.bufs[self.i][k]


class Pipe:
    def __init__(self):
        self.q = []

    def push(self, stages):
        self.q.insert(0, [stages, 0])
        self._step()

    def _step(self):
        for item in list(self.q):
            item[0][item[1]]()
            item[1] += 1
        self.q = [it for it in self.q if it[1] < len(it[0])]

    def flush(self):
        while self.q:
            self._step()


class Res:
    __slots__ = ("name", "w", "r", "dsem", "dcnt")

    def __init__(self, name):
        self.name = name
        self.w = None
        self.r = {}
        self.dsem = None
        self.dcnt = 0


class Builder:
    ENGS = ("sync", "tensor", "vector", "scalar", "gpsimd")

    def __init__(self, nc):
        self.nc = nc
        self.ops = {e: [] for e in self.ENGS}
        self.sems = {e: nc.alloc_semaphore("s_" + e) for e in self.ENGS}
        self.cnt = {e: 0 for e in self.ENGS}
        self.known = {e: {} for e in self.ENGS}
        self.dres = []
        self.nres = 0
        self.nins = 0
        self.arena = None
        self.nops = 0
        self.stop_ops = -1
        self.lp = self.rp = 0
        self.peak = 0

    def sb(self, name, shape, dtype, stack=None, side=None):
        if self.arena is None:
            total = (self.nc.sbuf_bytes_remaining - 4096) // 64 * 64
            total = min(total, 218 * 1024)
            self.arena = self.nc.alloc_sbuf_tensor("arena", [128, total // 2], BF16)
            self.lp, self.rp = 0, total
        n = 1
        for d in shape[1:]:
            n *= d
        esz = 2 if dtype == BF16 else 4
        nbytes = (n * esz + 63) // 64 * 64
        if side == "right":
            self.rp -= nbytes
            off = self.rp
            if stack is not None:
                stack.callback(self._restore, "rp", off + nbytes)
        else:
            off = self.lp
            self.lp += nbytes
            if stack is not None:
                stack.callback(self._restore, "lp", off)
        assert self.lp <= self.rp, ("SBUF arena exhausted", name, self.lp, self.rp)
        self.peak = max(self.peak, self.lp + (self.arena.shape[1] * 2 - self.rp))
        ap = self.arena[0:shape[0], off // 2:(off + n * esz) // 2]
        if dtype != BF16:
            ap = ap.bitcast(dtype)
        if len(shape) == 3:
            ap = ap.rearrange("p (a b) -> p a b", a=shape[1], b=shape[2])
        elif len(shape) == 4:
            ap = ap.rearrange("p (a b c) -> p a b c", a=shape[1], b=shape[2], c=shape[3])
        return ap

    def _restore(self, which, val):
        setattr(self, which, val)

    def res(self, name=None):
        self.nres += 1
        return Res(name or ("r%d" % self.nres))

    def _wait(self, eng, toks):
        k = self.known[eng]
        own = self.sems[eng] if eng == "tensor" else None
        for (sem, val) in toks:
            if sem is own:
                continue
            if k.get(sem, 0) < val:
                k[sem] = val
                self.ops[eng].append(("wait", sem, val))

    def _deps(self, eng, reads, writes):
        toks = []
        for r in reads:
            if r.w is not None:
                toks.append(r.w)
        for w in writes:
            if w.w is not None:
                toks.append(w.w)
            toks.extend(w.r.items())
        self._wait(eng, toks)

    def _flush(self):
        for ename in self.ENGS:
            lst = self.ops[ename]
            if not lst:
                continue
            e = getattr(self.nc, ename)
            sem = self.sems[ename]
            for o in lst:
                if o[0] == "wait":
                    e.wait_ge(o[1], o[2])
                elif o[0] == "ins":
                    ins = o[1](e)
                    if o[2]:
                        ins.then_inc(sem, 1)
                else:
                    e.dma_start(out=o[1], in_=o[2]).then_inc(o[3], 16)
                self.nins += 1
            self.ops[ename] = []

    def _update(self, tok, reads, writes):
        for r in reads:
            if r.r.get(tok[0], 0) < tok[1]:
                r.r[tok[0]] = tok[1]
        for w in writes:
            w.w = tok
            w.r = {}

    def op(self, eng, fns, reads=(), writes=()):
        if not isinstance(fns, (list, tuple)):
            fns = [fns]
        self.nops += 1
        if self.nops == self.stop_ops:
            raise _Stop()
        self._deps(eng, reads, writes)
        s = self.sems[eng]
        self.cnt[eng] += 1
        tok = (s, self.cnt[eng])
        n = len(fns)
        for i, fn in enumerate(fns):
            self.ops[eng].append(("ins", fn, i == n - 1))
        self._update(tok, reads, writes)
        self._flush()
        return tok

    def dma(self, out_ap, in_ap, reads=(), writes=(), q="sync", semres=None):
        res = semres if semres is not None else (writes[0] if writes else reads[0])
        if res.dsem is None:
            res.dsem = self.nc.alloc_semaphore("d%d" % len(self.dres))
            self.dres.append(res)
        self._deps(q, reads, writes)
        if res.dcnt > 0:
            self._wait(q, [(res.dsem, res.dcnt * 16)])
        res.dcnt += 1
        tok = (res.dsem, res.dcnt * 16)
        self.ops[q].append(("dma", out_ap, in_ap, res.dsem))
        self._update(tok, reads, writes)
        self._flush()
        return tok

    def barrier(self):
        toks = [(self.sems[e], self.cnt[e]) for e in self.ENGS if self.cnt[e] > 0]
        toks += [(r.dsem, r.dcnt * 16) for r in self.dres]
        for e in self.ENGS:
            own = self.sems[e]
            k = self.known[e]
            for (sem, val) in toks:
                if k.get(sem, 0) < val:
                    k[sem] = val
                    self.ops[e].append(("wait", sem, val))
        self._flush()

    def emit(self):
        self.barrier()
        self._flush()


def build_program(debug=False, stop_after=99, stop_ops=-1):
    nc = bass.Bass("TRN2", target_bir_lowering=False, dynamic_dma_scratch_size=256)
    B = Builder(nc)
    B.stop_ops = stop_ops

    def din(name, shape, dt=F32):
        return nc.dram_tensor(name, list(shape), dt, kind="ExternalInput").ap()

    def dout(name, shape, dt=F32):
        return nc.dram_tensor(name, list(shape), dt, kind="ExternalOutput").ap()

    xa = din("xa", [OWN + NB, 2048])
    cs_d = din("cs", [128, 17, 16])
    cmask_d = din("cmask", [128, 8, 128])
    mem_x = din("mem_x", [256, 2048])
    c_dak = din("c_dak", [1024, 1024])
    c_dav = din("c_dav", [1024, 1024])
    c_mk = din("c_mk", [256, 512])
    c_mv = din("c_mv", [256, 512])
    w_in_b = din("w_in_b", [84, 128, 16, 128])
    w_br_b = din("w_br_b", [16, 128, 16, 128])
    w_out_b = din("w_out_b", [16, 128, 16, 128])
    w_pq_b = din("w_pq_b", [16, 128, 16, 128])
    w_mkv_b = din("w_mkv_b", [8, 128, 16, 128])
    peer_ut = din("peer_ut", [128, 128, 16, 128])
    peer_v = din("peer_v", [16384, 2048])
    pk_t = din("pk_t", [128, 16, 128])
    vecs = din("vecs", [128, 6144 + NSV])
    bgT = din("bgT", [128, 48])
    bsT = din("bsT", [128, 4])
    wsl = din("wsl", [128, 4, 128])

    y_o = dout("y", [OWN, 2048])
    ok_o = dout("ok", [OWN, 1024])
    ov_o = dout("ov", [OWN, 1024])
    omk_o = dout("omk", [256, 512])
    omv_o = dout("omv", [256, 512])
    ocv_o = dout("ocv", [64, 512])
    if debug:
        x1_o = dout("x1dbg", [OWN, 2048])
        dbg_br = dout("dbg_br", [128, 16, OWN], BF16)
        dbg_mg = dout("dbg_mg", [128, 16, OWN], BF16)
    x1d = nc.dram_tensor("x1d", [OWN, 2048], F32, kind="Internal").ap()
    wd = nc.dram_tensor("wd", [128, 128, 1152], BF16, kind="Internal").ap()
    gad = nc.dram_tensor("gad", [128, 128, OWN], BF16, kind="Internal").ap()
    RGAD = B.res("gad")
    RX1D = B.res("x1d")
    RWD = B.res("wd")

    ps = [nc.alloc_psum_tensor("ps%d" % i, [128, 512], F32) for i in range(8)]
    PR = [B.res("ps%d" % i) for i in range(8)]
    rot = {}

    def bank(ids):
        k = tuple(ids)
        i = rot.get(k, 0)
        rot[k] = i + 1
        b = ids[i % len(ids)]
        return ps[b], PR[b]

    def cp(eng, out, in_, reads, writes):
        if eng == "scalar":
            return B.op(eng, lambda e: e.copy(out=out, in_=in_), reads, writes)
        return B.op(eng, lambda e: e.tensor_copy(out=out, in_=in_), reads, writes)

    TBA = [(i * 128, 128) for i in range(8)]
    TBS = [(1024, 64)]
    TBB = [(1088 + i * 128, 128) for i in range(8)]
    TB_OWN = TBA + TBS
    TB_ALL = TBA + TBS + TBB
    TSL = [(0, 512), (512, 512), (1024, 64)]

    C = ExitStack()
    ident_bf = B.sb("ident_bf", [128, 128], BF16)
    ident_f = B.sb("ident_f", [128, 128], F32)
    iota128 = B.sb("iota128", [128, 128], F32)
    sv = B.sb("sv", [128, NSV], F32)
    neglam = B.sb("neglam", [128, 1], F32)
    gkq = B.sb("gkq", [128, 4, 64], F32)
    gout08 = B.sb("gout08", [128, 128], F32)
    NSS = 12
    ss = _RotBuf([B.sb("ss", [128, 8], F32) for _ in range(NSS)])
    ss2 = _RotBuf([B.sb("ss2", [128, 8], F32) for _ in range(NSS)])
    ss_res = [(B.res(), B.res()) for _ in range(NSS)]
    epsb = B.sb("epsb", [128, 1], F32)
    JS = ExitStack()
    junk = B.sb("junk", [128, 2048], BF16, JS)
    R = {n: B.res(n) for n in ["ident", "iota", "sv", "neglam", "gkq", "gout08", "ss", "ss2", "junk", "epsb"]}

    def rot_ss():
        ss.i = ss2.i = (ss.i + 1) % NSS
        R["ss"], R["ss2"] = ss_res[ss.i]

    rot_ss()
    B.op("vector", lambda e: e.memset(epsb[:], EPS), writes=[R["epsb"]])
    with ExitStack() as s0:
        rowi = B.sb("rowi", [128, 128], I32, s0)
        coli = B.sb("coli", [128, 128], I32, s0)
        rowf = B.sb("rowf", [128, 128], F32, s0)
        lt = B.sb("lt", [128, 128], F32, s0)
        r_rowi, r_coli, r_rowf, r_lt = B.res(), B.res(), B.res(), B.res()
        B.op("gpsimd", lambda e: e.iota(rowi[:], [[0, 128]], base=0, channel_multiplier=1), writes=[r_rowi])
        B.op("gpsimd", lambda e: e.iota(coli[:], [[1, 128]], base=0, channel_multiplier=0), writes=[r_coli])
        cp("vector", rowf[:], rowi[:], [r_rowi], [r_rowf])
        cp("vector", iota128[:], coli[:], [r_coli], [R["iota"]])
        B.op("vector", lambda e: e.tensor_tensor(out=ident_f[:], in0=rowf[:], in1=iota128[:], op=ALU.is_equal),
             [r_rowf, R["iota"]], [R["ident"]])
        cp("vector", ident_bf[:], ident_f[:], [R["ident"]], [R["ident"]])
        B.dma(sv[:], vecs[:, 6144:6144 + NSV], writes=[R["sv"]])
        B.op("vector", lambda e: e.tensor_tensor(out=lt[:, 0:64], in0=sv[:, LQ:LQ + 64], in1=sv[:, LQ + 64:LQ + 128], op=ALU.mult),
             [R["sv"]], [r_lt])
        B.op("vector", lambda e: e.tensor_tensor(out=lt[:, 64:128], in0=sv[:, LQ + 128:LQ + 192], in1=sv[:, LQ + 192:LQ + 256], op=ALU.mult),
             [R["sv"]], [r_lt])
        B.op("vector", lambda e: e.tensor_reduce(out=ss[:, 0:2], in_=lt[:].rearrange("p (a b) -> p a b", a=2), axis=AX.X, op=ALU.add),
             [r_lt], [R["ss"]])
        B.op("scalar", lambda e: e.activation(out=ss2[:, 0:2], in_=ss[:, 0:2], func=AF.Exp), [R["ss"]], [R["ss2"]])
        B.op("vector", lambda e: e.tensor_tensor(out=neglam[:], in0=ss2[:, 1:2], in1=ss2[:, 0:1], op=ALU.subtract),
             [R["ss2"]], [R["neglam"]])
        B.op("vector", lambda e: e.tensor_scalar(out=neglam[:], in0=neglam[:], scalar1=-0.2, scalar2=None, op0=ALU.add),
             [R["neglam"]], [R["neglam"]])
        for gi, off in enumerate([GK, GK, GQ, GQ]):
            B.op("vector", lambda e: e.tensor_scalar(out=gkq[:, gi, :], in0=sv[:, off:off + 64], scalar1=(1.0 if gi < 2 else 0.125), scalar2=None,
                                                     op0=ALU.mult), [R["sv"]], [R["gkq"]])
        B.op("vector", lambda e: e.tensor_scalar(out=gout08[:], in0=sv[:, GOUT:GOUT + 128], scalar1=0.8, scalar2=None, op0=ALU.mult),
             [R["sv"]], [R["gout08"]])
        B.barrier()

    def take_ss():
        rot_ss()
        i = ss.i
        return (ss.bufs[i], ss2.bufs[i], ss_res[i][0], ss_res[i][1])

    def rstd_from_ss(n, ncol, D, scale=None, ssb=None):
        if ssb is None:
            ssb = (ss.bufs[ss.i], ss2.bufs[ss.i], R["ss"], R["ss2"])
        s_, s2_, rs, rs2 = ssb
        B.op("scalar", lambda e: e.activation(out=s2_[0:n, 0:ncol], in_=s_[0:n, 0:ncol], func=AF.Ln, scale=1.0 / D, bias=epsb[0:n, 0:1]),
             [rs, R["epsb"]], [rs2])
        B.op("scalar", lambda e: e.activation(out=s2_[0:n, 0:ncol], in_=s2_[0:n, 0:ncol], func=AF.Exp, scale=-0.5), [rs2], [rs2])
        if scale is not None:
            c0, c1, sc = scale
            B.op("vector", lambda e: e.tensor_scalar(out=s2_[0:n, c0:c1], in0=s2_[0:n, c0:c1], scalar1=sc, scalar2=None, op0=ALU.mult),
                 [rs2], [rs2])

    def rms_rows(src, n, D, g_ap, g_res, dst, reads, writes):
        rot_ss()
        B.op("scalar", lambda e: e.activation(out=junk[0:n, 0:D], in_=src, func=AF.Square, accum_out=ss[0:n, 0:1]),
             reads, [B.res(), R["ss"]])
        rstd_from_ss(n, 1, D)
        B.op("vector", lambda e: e.scalar_tensor_tensor(out=dst, in0=src, scalar=ss2[0:n, 0:1], in1=g_ap, op0=ALU.mult, op1=ALU.mult),
             list(reads) + [R["ss2"], g_res], writes)

    def rms_rows_stages(src, n, D, g_ap, g_res, dst, reads, writes):
        ssb = take_ss()

        def A():
            B.op("scalar", lambda e: e.activation(out=junk[0:n, 0:D], in_=src, func=AF.Square, accum_out=ssb[0][0:n, 0:1]),
                 reads, [B.res(), ssb[2]])
            rstd_from_ss(n, 1, D, ssb=ssb)

        def Bs():
            B.op("vector", lambda e: e.scalar_tensor_tensor(out=dst, in0=src, scalar=ssb[1][0:n, 0:1], in1=g_ap, op0=ALU.mult, op1=ALU.mult),
                 list(reads) + [ssb[3], g_res], writes)
        return A, Bs

    def tr_bf(items, n_in, reads, banks):
        pt, pr = bank(banks)
        pb = pt[:].bitcast(BF16)
        fns = [(lambda e, j=j, a=a: e.transpose(pb[:, j * 128:j * 128 + n_in], a, ident_bf[0:n_in, 0:n_in])) for j, a in enumerate(items)]
        B.op("tensor", fns, list(reads) + [R["ident"]], [pr])
        v = pb[:, 0:len(items) * 128].rearrange("p (j t) -> p j t", t=128)[:, :, 0:n_in]
        return v, pr

    WST = ExitStack()
    NST = 2
    stage = [B.sb("stage%d" % i, [128, 2048], F32, WST) for i in range(NST)]
    stage_r = [B.res("stage%d" % i) for i in range(NST)]
    wl = [0]

    def wload(src, dst, dst_res, eng=None):
        i = wl[0] % NST
        if eng is None:
            eng = "gpsimd" if wl[0] % 2 == 0 else "scalar"
        wl[0] += 1
        nel = 1
        for d in src.shape[1:]:
            nel *= d
        st = stage[i][:, 0:nel]
        if len(src.shape) == 3:
            st = st.rearrange("p (a b) -> p a b", a=src.shape[1])
        B.dma(st, src, writes=[stage_r[i]])
        cp(eng, dst, st, [stage_r[i]], [dst_res])

    try:
        P03 = ExitStack()
        hT_own = B.sb("hT_own", [128, 16, OWN], BF16, P03)
        o_daT = B.sb("o_daT", [128, 8, OWN], BF16, P03)
        o_cmT = B.sb("o_cmT", [128, 4, OWN], BF16, P03)
        o_memT = B.sb("o_memT", [128, 4, OWN], BF16, P03)
        R_hTo, R_odaT, R_ocmT, R_omemT = B.res("hTo"), B.res("odaT"), B.res("ocmT"), B.res("omemT")
        PR1 = ExitStack()
        hT_B = B.sb("hT_B", [128, 16, NB], BF16, PR1, side="right")
        R_hTB = B.res("hTB")

        def hT(dc, c0, n):
            if c0 < OWN:
                return hT_own[:, dc, c0:c0 + n], R_hTo
            return hT_B[:, dc, c0 - OWN:c0 - OWN + n], R_hTB

        with ExitStack() as s0:
            gmix = B.sb("gmix", [128, 2048], F32, s0)
            R_gmix = B.res()
            B.dma(gmix[:], vecs[:, 0:2048], writes=[R_gmix])
            NX0 = 4
            xt = [B.sb("xt%d" % i, [128, 2048], F32, s0) for i in range(NX0)]
            hn = [B.sb("hn%d" % i, [128, 2048], BF16, s0) for i in range(NX0)]
            R_xt = [B.res() for _ in range(NX0)]
            R_hn = [B.res() for _ in range(NX0)]
            p0 = Pipe()
            for bi, (c0, n) in enumerate(TB_ALL):
                def mk0(bi=bi, c0=c0, n=n):
                    x_, hn_, rx, rh = xt[bi % NX0], hn[bi % NX0], R_xt[bi % NX0], R_hn[bi % NX0]
                    box = {}

                    def S0():
                        B.dma(x_[0:n, :], xa[c0:c0 + n, :], writes=[rx])
                        box["st"] = rms_rows_stages(x_[0:n, :], n, 2048, gmix[0:n, :], R_gmix, hn_[0:n, :], [rx], [rh])

                    def S1():
                        box["st"][0]()

                    def S2():
                        box["st"][1]()

                    def S3():
                        box["tr"] = []
                        for half in range(2):
                            items = [hn_[0:n, (half * 8 + j) * 128:(half * 8 + j + 1) * 128] for j in range(8)]
                            box["tr"].append(tr_bf(items, n, [rh], [0, 1, 2, 3, 4, 5, 6, 7]))

                    def S4():
                        for half in range(2):
                            v, pr = box["tr"][half]
                            if c0 < OWN:
                                dst, dr = hT_own[:, half * 8:half * 8 + 8, c0:c0 + n], R_hTo
                            else:
                                dst, dr = hT_B[:, half * 8:half * 8 + 8, c0 - OWN:c0 - OWN + n], R_hTB
                            cp("scalar" if half == 0 else "vector", dst, v, [pr], [dr])
                    return [S0, S1, S2, S3, S4]
                p0.push(mk0())
            p0.flush()
            B.barrier()
        if stop_after <= 0:
            raise _Stop()

        PROJ, TRB, SCB, OB0, OB1 = [0, 1], [2, 7], [3, 4], [5], [6]
        with ExitStack() as s1:
            HS = 2
            wk = [B.sb("wk", [128, 16, 128], BF16, s1) for _ in range(HS)]
            wv = [B.sb("wv", [128, 16, 128], BF16, s1) for _ in range(HS)]
            wq = [B.sb("wq", [128, 16, 128], BF16, s1) for _ in range(HS)]
            R_wk, R_wv, R_wq = [B.res() for _ in range(HS)], [B.res() for _ in range(HS)], [B.res() for _ in range(HS)]
            kT = [B.sb("kT", [128, OWN + NB], BF16, s1) for _ in range(HS)]
            kTs = [B.sb("kTs", [128, 1024], BF16, s1) for _ in range(HS)]
            Vb = [B.sb("Vb", [128, 17, 130], BF16, s1) for _ in range(HS)]
            Vs = [B.sb("Vs", [128, 8, 130], BF16, s1) for _ in range(HS)]
            qT = [B.sb("qT", [128, 2, OWN], BF16, s1) for _ in range(HS)]
            R_kT, R_kTs, R_Vb, R_Vs, R_qT = ([B.res() for _ in range(HS)] for _ in range(5))
            cs = B.sb("cs", [128, 17, 16], F32, s1)
            cm = B.sb("cm", [128, 8, 128], F32, s1)
            R_cs, R_cm = B.res("cs"), B.res("cm")
            B.dma(cs[:], cs_d[:, :, :], writes=[R_cs])
            B.dma(cm[:], cmask_d[:, :, :], writes=[R_cm])
            for s_ in range(HS):
                B.op("vector", lambda e: e.memset(qT[s_][:], 0.0), writes=[R_qT[s_]])
                B.op("gpsimd", lambda e: e.memset(Vb[s_][:, :, 128:130], 1.0), writes=[R_Vb[s_]])
                B.op("gpsimd", lambda e: e.memset(Vs[s_][:, :, 128:130], 1.0), writes=[R_Vs[s_]])
            ckb = B.sb("ckb", [128, 8, 128], BF16, s1)
            R_ckb = B.res("ckb")
            NKQ = 8
            sq = [B.sb("sq", [128, 256], F32, s1) for _ in range(3)]
            R_sq = [B.res() for _ in range(3)]
            kq = [B.sb("kq%d" % i, [128, 4, 64], F32, s1) for i in range(NKQ)]
            R_kq = [B.res("kq%d" % i) for i in range(NKQ)]
            rt = [B.sb("rt", [128, 4, 4, 8], F32, s1) for _ in range(2)]
            R_rt = [B.res(), B.res()]
            kqb = [B.sb("kqb", [128, 256], BF16, s1) for _ in range(3)]
            R_kqb = [B.res() for _ in range(3)]
            vf = [B.sb("vf%d" % i, [128, 128], F32, s1) for i in range(3)]
            R_vf = [B.res("vf%d" % i) for i in range(3)]
            NPT = 4
            PT = [B.sb("PT%d" % i, [128, 256], BF16, s1) for i in range(NPT)]
            R_PT = [B.res("PT%d" % i) for i in range(NPT)]
            NFZ = 4
            rz = [B.sb("rz", [128, 4], F32, s1) for _ in range(NFZ)]
            R_rz = [B.res() for _ in range(NFZ)]
            of = [B.sb("of", [128, 128], F32, s1) for _ in range(NFZ)]
            R_of = [B.res() for _ in range(NFZ)]
            ob = [B.sb("ob", [128, 128], BF16, s1) for _ in range(NFZ)]
            R_ob = [B.res() for _ in range(NFZ)]
            pp = Pipe()
            ptc = [0]
            kqc = [0]
            fzc = [0]

            def da_finalize(ots, orrs, n, h, qc0):
                ot, ot1 = ots
                orr, orr1 = orrs
                f = fzc[0] % NFZ
                fzc[0] += 1
                rz_, rrz, of_, rof, ob_, rob = rz[f], R_rz[f], of[f], R_of[f], ob[f], R_ob[f]
                ssb = take_ss()
                box = {}

                def F1():
                    B.op("vector", lambda e: e.reciprocal(out=rz_[0:n, 0:1], in_=ot[0:n, 128:129]), [orr], [rrz])
                    B.op("vector", lambda e: e.reciprocal(out=rz_[0:n, 1:2], in_=ot1[0:n, 128:129]), [orr1], [rrz])
                    B.op("vector", lambda e: e.tensor_tensor(out=rz_[0:n, 2:3], in0=rz_[0:n, 1:2], in1=neglam[0:n, 0:1], op=ALU.mult),
                         [rrz, R["neglam"]], [rrz])
                    B.op("vector", lambda e: e.tensor_scalar(out=of_[0:n, :], in0=ot[0:n, 0:128], scalar1=rz_[0:n, 0:1], scalar2=None, op0=ALU.mult),
                         [orr, rrz], [rof])
                    B.op("vector", lambda e: e.scalar_tensor_tensor(out=of_[0:n, :], in0=ot1[0:n, 0:128], scalar=rz_[0:n, 2:3], in1=of_[0:n, :],
                                                                    op0=ALU.mult, op1=ALU.add), [orr1, rrz, rof], [rof])

                def F2():
                    B.op("scalar", lambda e: e.activation(out=junk[0:n, 0:128], in_=of_[0:n, :], func=AF.Square, accum_out=ssb[0][0:n, 0:1]),
                         [rof], [B.res(), ssb[2]])
                    rstd_from_ss(n, 1, 128, ssb=ssb)

                def F3():
                    B.op("vector", lambda e: e.scalar_tensor_tensor(out=ob_[0:n, :], in0=of_[0:n, :], scalar=ssb[1][0:n, 0:1], in1=gout08[0:n, :],
                                                                    op0=ALU.mult, op1=ALU.mult), [rof, ssb[3], R["gout08"]], [rob])

                def F4():
                    box["tr"] = tr_bf([ob_[0:n, :]], n, [rob], TRB)

                def F5():
                    v, pr = box["tr"]
                    cp("scalar", o_daT[:, h, qc0:qc0 + n], v[:, 0, :], [pr], [R_odaT])

                pp.push([F1, F2, F3, F4, F5])

            def head_setup(h):
                s_ = h % HS
                wload(w_in_b[8 + h], wk[s_][:], R_wk[s_])
                wload(w_in_b[16 + h], wv[s_][:], R_wv[s_])
                wload(w_in_b[h], wq[s_][:], R_wq[s_])
                wload(c_dak.rearrange("(b p) c -> p b c", p=128)[:, :, h * 128:(h + 1) * 128], ckb[:], R_ckb)
                wload(c_dav.rearrange("(b p) c -> p b c", p=128)[:, :, h * 128:(h + 1) * 128], Vs[s_][:, :, 0:128], R_Vs[s_])
                v, pr = tr_bf([ckb[:, j, :] for j in range(8)], 128, [R_ckb], TRB)
                cp("scalar", kTs[s_][:].rearrange("p (j t) -> p j t", t=128), v, [pr], [R_kTs[s_]])

            def proj_stages(h, bi):
                s_ = h % HS
                c0, n = TB_ALL[bi]
                own = c0 < OWN
                ng = 4 if own else 2
                w_ = ng * 64
                kc = kqc[0]
                kqc[0] += 1
                kq_, rkq = kq[kc % NKQ], R_kq[kc % NKQ]
                vf_, rvf = vf[kc % 3], R_vf[kc % 3]
                sq_, rsq = sq[kc % 3], R_sq[kc % 3]
                rt_, rrt = rt[kc % 2], R_rt[kc % 2]
                kqb_, rkqb = kqb[kc % 3], R_kqb[kc % 3]
                st8 = {}

                def S1():
                    pt, pr = bank(PROJ)
                    st8["pt"] = (pt, pr)
                    fns = []
                    groups = [(0, wk[s_]), (256, wv[s_])] + ([(128, wq[s_])] if own else [])
                    for (pc0, wt) in groups:
                        for dc in range(16):
                            a, rh = hT(dc, c0, n)
                            fns.append(lambda e, a=a, wt=wt, dc=dc, pc0=pc0: e.matmul(pt[0:n, pc0:pc0 + 128], lhsT=a, rhs=wt[:, dc, :],
                                                                                    start=(dc == 0), stop=(dc == 15)))
                    B.op("tensor", fns, [rh, R_wk[s_], R_wv[s_], R_wq[s_]], [pr])

                def S2():
                    pt, pr = st8["pt"]
                    B.op("scalar", lambda e: e.activation(out=sq_[0:n, 0:w_], in_=pt[0:n, 0:w_], func=AF.Square), [pr], [rsq])
                    cp("scalar", kq_[0:n, 0:ng, :], pt[0:n, 0:w_].rearrange("p (g d) -> p g d", d=64), [pr], [rkq])
                    cp("scalar", Vb[s_][0:n, bi, 0:128], pt[0:n, 256:384], [pr], [R_Vb[s_]])
                    if own:
                        cp("scalar", vf_[0:n, :], pt[0:n, 256:384], [pr], [rvf])
                        B.dma(ov_o[c0:c0 + n, h * 128:(h + 1) * 128], vf_[0:n, :], reads=[rvf], q="scalar")

                def S3():
                    ssb = take_ss()
                    st8["ssb"] = ssb
                    B.op("vector", lambda e: e.tensor_reduce(out=ssb[0][0:n, 0:ng], in_=sq_[0:n, 0:w_].rearrange("p (g d) -> p g d", d=64),
                                                             axis=AX.X, op=ALU.add), [rsq], [ssb[2]])

                def S4():
                    rstd_from_ss(n, ng, 64, ssb=st8["ssb"])

                def S5():
                    ssb = st8["ssb"]
                    B.op("vector", lambda e: e.tensor_tensor(out=kq_[0:n, 0:ng, :], in0=kq_[0:n, 0:ng, :],
                                                             in1=ssb[1][0:n, 0:ng].unsqueeze(2).to_broadcast([n, ng, 64]), op=ALU.mult),
                         [rkq, ssb[3]], [rkq])
                    B.op("vector", lambda e: e.tensor_tensor(out=kq_[0:n, 0:ng, :], in0=kq_[0:n, 0:ng, :], in1=gkq[0:n, 0:ng, :], op=ALU.mult),
                         [rkq, R["gkq"]], [rkq])
                    x1 = kq_[0:n, 0:ng, 0:8]
                    x2 = kq_[0:n, 0:ng, 8:16]
                    cosb = cs[0:n, bi, 0:8].unsqueeze(1).to_broadcast([n, ng, 8])
                    sinb = cs[0:n, bi, 8:16].unsqueeze(1).to_broadcast([n, ng, 8])
                    for ti, (xx, tb_) in enumerate([(x1, cosb), (x2, sinb), (x2, cosb), (x1, sinb)]):
                        B.op("vector", lambda e: e.tensor_tensor(out=rt_[0:n, ti, 0:ng, :], in0=xx, in1=tb_, op=ALU.mult), [rkq, R_cs], [rrt])
                    B.op("vector", lambda e: e.tensor_tensor(out=x1, in0=rt_[0:n, 0, 0:ng, :], in1=rt_[0:n, 1, 0:ng, :], op=ALU.subtract), [rrt], [rkq])
                    B.op("vector", lambda e: e.tensor_tensor(out=x2, in0=rt_[0:n, 2, 0:ng, :], in1=rt_[0:n, 3, 0:ng, :], op=ALU.add), [rrt], [rkq])

                def S6():
                    cp("scalar", kqb_[0:n, 0:w_], kq_[0:n, 0:ng, :].rearrange("p g d -> p (g d)"), [rkq], [rkqb])
                    if own:
                        B.dma(ok_o[c0:c0 + n, h * 128:(h + 1) * 128], kq_[0:n, 0:2, :].rearrange("p g d -> p (g d)"), reads=[rkq], q="scalar")

                def S7():
                    items = [kqb_[0:n, 0:128]] + ([kqb_[0:n, 128:256]] if own else [])
                    st8["tr"] = tr_bf(items, n, [rkqb], TRB)

                def S8():
                    v, prt = st8["tr"]
                    cp("vector", kT[s_][:, c0:c0 + n], v[:, 0, :], [prt], [R_kT[s_]])
                    if own:
                        cp("vector", qT[s_][0:64, 0, c0:c0 + n], v[0:64, 1, :], [prt], [R_qT[s_]])
                        cp("vector", qT[s_][64:128, 1, c0:c0 + n], v[64:128, 1, :], [prt], [R_qT[s_]])

                return [S1, S2, S3, S4, S5, S6, S7, S8]

            def attn_unit(h, i):
                s_ = h % HS
                ot0, orr0 = bank(OB0)
                ot1, orr1 = bank(OB1)
                ots, orrs = (ot0, ot1), (orr0, orr1)
                ap_ = Pipe()
                if i < 8:
                    keys = [("A", j) for j in range(i + 1)] + [("B", j) for j in range(i + 1)]
                    nkeys = len(keys)
                    nq, q0 = 128, i * 128
                else:
                    keys = [("P", j) for j in range(8)] + [("N", 0)]
                    nkeys = 9
                    nq, q0 = 64, 1024
                for ki, (kind, j) in enumerate(keys):
                    def mk(ki=ki, kind=kind, j=j):
                        nk = 64 if kind == "N" else 128
                        if kind == "A":
                            lh, rv, rr = kT[s_][:, j * 128:(j + 1) * 128], Vb[s_][0:nk, j, 0:129], [R_kT[s_], R_Vb[s_]]
                        elif kind == "B":
                            lh, rv, rr = kT[s_][:, OWN + j * 128:OWN + (j + 1) * 128], Vb[s_][0:nk, 9 + j, 0:129], [R_kT[s_], R_Vb[s_]]
                        elif kind == "P":
                            lh, rv, rr = kTs[s_][:, j * 128:(j + 1) * 128], Vs[s_][0:nk, j, 0:129], [R_kTs[s_], R_Vs[s_]]
                        else:
                            lh, rv, rr = kT[s_][:, 1024:1088], Vb[s_][0:nk, 8, 0:129], [R_kT[s_], R_Vb[s_]]
                        box = {}

                        def A1():
                            st, sr = bank(SCB)
                            fns = [(lambda e, m=m: e.matmul(st[0:nk, m * nq:(m + 1) * nq], lhsT=lh, rhs=qT[s_][:, m, q0:q0 + nq],
                                                            start=True, stop=True)) for m in range(2)]
                            B.op("tensor", fns, [rr[0], R_qT[s_]], [sr])
                            P_, rp = PT[ptc[0] % NPT], R_PT[ptc[0] % NPT]
                            ptc[0] += 1
                            box["P"] = (P_, rp)
                            B.op("scalar", lambda e: e.activation(out=P_[0:nk, 0:2 * nq], in_=st[0:nk, 0:2 * nq], func=AF.Exp), [sr], [rp])
                            if i < 8 and j == i:
                                if kind == "A":
                                    B.op("gpsimd", lambda e: e.memset(P_[64:128, :].rearrange("p (m q) -> p m q", m=2)[:, :, 0:64], 0.0), [], [rp])
                                else:
                                    B.op("vector", lambda e: e.tensor_tensor(out=P_[:, :].rearrange("p (m q) -> p m q", m=2),
                                                                             in0=P_[:, :].rearrange("p (m q) -> p m q", m=2),
                                                                             in1=cm[:, i, :].unsqueeze(1).to_broadcast([128, 2, 128]), op=ALU.mult),
                                         [R_cm], [rp])

                        def A2():
                            P_, rp = box["P"]
                            fns = [(lambda e, m=m: e.matmul(ots[m][0:nq, 0:129], lhsT=P_[0:nk, m * nq:(m + 1) * nq], rhs=rv,
                                                            start=(ki == 0), stop=(ki == nkeys - 1))) for m in range(2)]
                            B.op("tensor", fns, [rp, rr[1]], [orr0, orr1])
                        return [A1, A2]
                    ap_.push(mk())
                ap_.flush()
                da_finalize(ots, orrs, nq, h, q0)

            head_setup(0)
            for bi in range(17):
                pp.push(proj_stages(0, bi))
            pp.flush()
            for h in range(8):
                if stop_after == 0.5 and h == 1:
                    raise _Stop()
                nxt = h + 1 < 8
                if nxt:
                    head_setup(h + 1)
                pb_i = 0
                for u in range(9):
                    attn_unit(h, u)
                    if nxt:
                        for _ in range(2 if u < 8 else 1):
                            pp.push(proj_stages(h + 1, pb_i))
                            pb_i += 1
                assert (not nxt) or pb_i == 17
                pp.flush()
            B.barrier()
        PR1.close()
        if stop_after <= 1:
            raise _Stop()

        with ExitStack() as s2:
            PROJ, TRB, SCB = [0, 1, 2], [3], [4, 5]
            mkT = [B.sb("mkT%d" % i, [128, 4, 256], BF16, s2) for i in range(2)]
            mvb = [B.sb("mvb%d" % i, [128, 2, 4, 130], BF16, s2) for i in range(2)]
            R_mkT = [B.res(), B.res()]
            R_mvb = [B.res(), B.res()]
            for i in range(2):
                B.op("gpsimd", lambda e, i=i: e.memset(mvb[i][:, :, :, 128:130], 1.0), writes=[R_mvb[i]])
            t512 = [B.sb("t512_%d" % i, [128, 512], F32, s2) for i in range(4)]
            R_t512 = [B.res() for i in range(4)]
            b512 = [B.sb("b512_%d" % i, [128, 512], BF16, s2) for i in range(3)]
            R_b512 = [B.res() for i in range(3)]
            with ExitStack() as s2a:
                gmem = B.sb("gmem", [128, 2048], F32, s2a)
                R_gmem = B.res()
                B.dma(gmem[:], vecs[:, 2048:4096], writes=[R_gmem])
                mxs = B.sb("mxs", [128, 2048], F32, s2a)
                mnb = B.sb("mnb", [128, 2048], BF16, s2a)
                mT = B.sb("mT", [128, 16, 256], BF16, s2a)
                R_mxs, R_mnb, R_mT = B.res(), B.res(), B.res()
                wmk = B.sb("wmk", [128, 16, 512], BF16, s2a)
                wmv = B.sb("wmv", [128, 16, 512], BF16, s2a)
                R_wmk, R_wmv = B.res(), B.res()
                for j in range(4):
                    wload(w_mkv_b[j], wmk[:, :, j * 128:(j + 1) * 128], R_wmk)
                for j in range(4):
                    wload(w_mkv_b[4 + j], wmv[:, :, j * 128:(j + 1) * 128], R_wmv)
                for mb in range(2):
                    B.dma(mxs[:], mem_x[mb * 128:(mb + 1) * 128, :], writes=[R_mxs])
                    rms_rows(mxs[:], 128, 2048, gmem[:], R_gmem, mnb[:], [R_mxs], [R_mnb])
                    for half in range(2):
                        v, pr = tr_bf([mnb[:, (half * 8 + j) * 128:(half * 8 + j + 1) * 128] for j in range(8)], 128, [R_mnb], TRB)
                        cp("scalar" if half == 0 else "vector", mT[:, half * 8:half * 8 + 8, mb * 128:(mb + 1) * 128], v, [pr], [R_mT])
                for mb in range(2):
                    pk, prk = bank(PROJ)
                    B.op("tensor", [(lambda e, dc=dc: e.matmul(pk[:, :], lhsT=mT[:, dc, mb * 128:(mb + 1) * 128], rhs=wmk[:, dc, :],
                                                               start=(dc == 0), stop=(dc == 15))) for dc in range(16)], [R_mT, R_wmk], [prk])
                    pv, prv = bank(PROJ)
                    B.op("tensor", [(lambda e, dc=dc: e.matmul(pv[:, :], lhsT=mT[:, dc, mb * 128:(mb + 1) * 128], rhs=wmv[:, dc, :],
                                                               start=(dc == 0), stop=(dc == 15))) for dc in range(16)], [R_mT, R_wmv], [prv])
                    tq, rtq = t512[0], R_t512[0]
                    tk, rtk = t512[1], R_t512[1]
                    B.op("scalar", lambda e: e.activation(out=tq[:, :], in_=pk[:, :], func=AF.Square), [prk], [rtq])
                    B.op("vector", lambda e: e.tensor_reduce(out=ss[:, 0:4], in_=tq[:, :].rearrange("p (g d) -> p g d", d=128), axis=AX.X, op=ALU.add),
                         [rtq], [R["ss"]])
                    rstd_from_ss(128, 4, 128)
                    B.op("vector", lambda e: e.tensor_tensor(out=tk[:, :].rearrange("p (g d) -> p g d", d=128),
                                                             in0=pk[:, :].rearrange("p (g d) -> p g d", d=128),
                                                             in1=ss2[:, 0:4].unsqueeze(2).to_broadcast([128, 4, 128]), op=ALU.mult), [prk, R["ss2"]], [rtk])
                    B.op(POOL_TT, lambda e: e.tensor_tensor(out=tk[:, :].rearrange("p (g d) -> p g d", d=128),
                                                             in0=tk[:, :].rearrange("p (g d) -> p g d", d=128),
                                                             in1=sv[:, MKG:MKG + 128].unsqueeze(1).to_broadcast([128, 4, 128]), op=ALU.mult),
                         [rtk, R["sv"]], [rtk])
                    B.dma(omk_o[mb * 128:(mb + 1) * 128, :], tk[:, :], reads=[rtk], q="scalar")
                    cp("scalar", b512[0][:, :], tk[:, :], [rtk], [R_b512[0]])
                    v, pr = tr_bf([b512[0][:, g * 128:(g + 1) * 128] for g in range(4)], 128, [R_b512[0]], TRB)
                    cp("vector", mkT[0][:, :, mb * 128:(mb + 1) * 128], v, [pr], [R_mkT[0]])
                    tv, rtv = t512[2], R_t512[2]
                    cp("vector", tv[:, :], pv[:, :], [prv], [rtv])
                    B.dma(omv_o[mb * 128:(mb + 1) * 128, :], tv[:, :], reads=[rtv], q="scalar")
                    cp("scalar", mvb[0][:, mb, :, 0:128], pv[:, :].rearrange("p (g d) -> p g d", d=128), [prv], [R_mvb[0]])
                B.barrier()
            for mb in range(2):
                tk, rtk = t512[1], R_t512[1]
                B.dma(tk[:, :], c_mk[mb * 128:(mb + 1) * 128, :], writes=[rtk])
                cp("gpsimd", b512[0][:, :], tk[:, :], [rtk], [R_b512[0]])
                v, pr = tr_bf([b512[0][:, g * 128:(g + 1) * 128] for g in range(4)], 128, [R_b512[0]], TRB)
                cp("vector", mkT[1][:, :, mb * 128:(mb + 1) * 128], v, [pr], [R_mkT[1]])
                tv, rtv = t512[2], R_t512[2]
                B.dma(tv[:, :], c_mv[mb * 128:(mb + 1) * 128, :], writes=[rtv])
                cp("gpsimd", mvb[1][:, mb, :, 0:128], tv[:, :].rearrange("p (g d) -> p g d", d=128), [rtv], [R_mvb[1]])
            wsf = B.sb("wsf", [128, 4, 128], F32, s2)
            wsb = B.sb("wsb", [128, 4, 128], BF16, s2)
            wsT = B.sb("wsT", [128, 4, 128], BF16, s2)
            bs_sb = B.sb("bs_sb", [128, 4], F32, s2)
            tril = B.sb("tril", [128, 128], F32, s2)
            rowf2 = B.sb("rowf2", [128, 1], F32, s2)
            rowi2 = B.sb("rowi2", [128, 1], I32, s2)
            R_wsf, R_wsb, R_wsT, R_bs, R_tril, R_row = B.res(), B.res(), B.res(), B.res(), B.res(), B.res()
            B.dma(wsf[:], wsl[:, :, :], writes=[R_wsf])
            B.dma(bs_sb[:], bsT[:, :], writes=[R_bs])
            B.op("gpsimd", lambda e: e.iota(rowi2[:], [[0, 1]], base=0, channel_multiplier=1), writes=[R_row])
            cp("vector", rowf2[:], rowi2[:], [R_row], [R_row])
            B.op("vector", lambda e: e.tensor_scalar(out=tril[:], in0=iota128[:], scalar1=rowf2[:, 0:1], scalar2=None, op0=ALU.is_le),
                 [R["iota"], R_row], [R_tril])
            B.op("vector", lambda e: e.tensor_tensor(out=wsb[:], in0=wsf[:], in1=tril[:].unsqueeze(1).to_broadcast([128, 4, 128]), op=ALU.mult),
                 [R_wsf, R_tril], [R_wsb])
            v, pr = tr_bf([wsb[:, g, :] for g in range(4)], 128, [R_wsb], TRB)
            cp("vector", wsT[:], v, [pr], [R_wsT])
            wu = B.sb("wu", [128, 16, 512], BF16, s2)
            wvc = B.sb("wvc", [128, 16, 512], BF16, s2)
            wqm = B.sb("wqm", [128, 16, 512], BF16, s2)
            R_wu, R_wvc, R_wqm = B.res(), B.res(), B.res()
            for j in range(4):
                wload(w_in_b[24 + j], wu[:, :, j * 128:(j + 1) * 128], R_wu)
                wload(w_in_b[28 + j], wvc[:, :, j * 128:(j + 1) * 128], R_wvc)
                wload(w_in_b[32 + j], wqm[:, :, j * 128:(j + 1) * 128], R_wqm)
            qmT = B.sb("qmT", [128, 4, 128], BF16, s2)
            R_qmT = B.res()
            PM = [B.sb("PM%d" % i, [128, 2, 128], BF16, s2) for i in range(2)]
            R_PM = [B.res(), B.res()]
            pmc = [0]
            for bi, (c0, n) in enumerate(TB_OWN):
                sm = 0 if bi < 8 else 1
                pu, pru = bank(PROJ)
                B.op("tensor", [(lambda e, dc=dc: e.matmul(pu[0:n, :], lhsT=hT_own[:, dc, c0:c0 + n], rhs=wu[:, dc, :],
                                                           start=(dc == 0), stop=(dc == 15))) for dc in range(16)], [R_hTo, R_wu], [pru])
                uf, ruf = t512[0], R_t512[0]
                B.op("scalar", lambda e: e.activation(out=uf[0:n, :], in_=pu[0:n, :], func=AF.Gelu), [pru], [ruf])
                pv, prv = bank(PROJ)
                B.op("tensor", [(lambda e, dc=dc: e.matmul(pv[0:n, :], lhsT=hT_own[:, dc, c0:c0 + n], rhs=wvc[:, dc, :],
                                                           start=(dc == 0), stop=(dc == 15))) for dc in range(16)], [R_hTo, R_wvc], [prv])
                vg, rvg = t512[1], R_t512[1]
                B.op("scalar", lambda e: e.activation(out=vg[0:n, :], in_=pv[0:n, :], func=AF.Gelu, accum_out=ss[0:n, 0:1]), [prv], [rvg, R["ss"]])
                B.op("vector", lambda e: e.tensor_scalar(out=ss2[0:n, 1:2], in0=ss[0:n, 0:1], scalar1=1.0 / 512, scalar2=None, op0=ALU.mult),
                     [R["ss"]], [R["ss2"]])
                B.op("vector", lambda e: e.tensor_scalar(out=vg[0:n, :], in0=vg[0:n, :], scalar1=ss2[0:n, 1:2], scalar2=None, op0=ALU.subtract),
                     [rvg, R["ss2"]], [rvg])
                B.op("scalar", lambda e: e.activation(out=junk[0:n, 0:512], in_=vg[0:n, :], func=AF.Square, accum_out=ss[0:n, 0:1]),
                     [rvg], [B.res(), R["ss"]])
                rstd_from_ss(n, 1, 512)
                vn, rvn = t512[2], R_t512[2]
                B.op("vector", lambda e: e.scalar_tensor_tensor(out=vn[0:n, :], in0=vg[0:n, :], scalar=ss2[0:n, 0:1], in1=sv[0:n, LNG:LNG + 512],
                                                                op0=ALU.mult, op1=ALU.mult), [rvg, R["ss2"], R["sv"]], [rvn])
                B.op(POOL_TT, lambda e: e.tensor_tensor(out=vn[0:n, :], in0=vn[0:n, :], in1=sv[0:n, LNB:LNB + 512], op=ALU.add), [rvn, R["sv"]], [rvn])
                if sm == 1:
                    B.dma(ocv_o[:, :], vn[0:n, :], reads=[rvn], q="scalar")
                cp("scalar", b512[0][0:n, :], vn[0:n, :], [rvn], [R_b512[0]])
                pc, prc = bank(SCB)
                B.op("tensor", [(lambda e, g=g: e.matmul(pc[0:n, g * 128:(g + 1) * 128], lhsT=wsT[0:n, g, 0:n], rhs=b512[0][0:n, g * 128:(g + 1) * 128],
                                                         start=True, stop=True)) for g in range(4)], [R_wsT, R_b512[0]], [prc])
                for g in range(4):
                    B.op("vector", lambda e, g=g: e.scalar_tensor_tensor(out=b512[1][0:n, g * 128:(g + 1) * 128], in0=pc[0:n, g * 128:(g + 1) * 128],
                                                                         scalar=bs_sb[0:n, g:g + 1], in1=uf[0:n, g * 128:(g + 1) * 128],
                                                                         op0=ALU.add, op1=ALU.mult), [prc, R_bs, ruf], [R_b512[1]])
                v, pr = tr_bf([b512[1][0:n, g * 128:(g + 1) * 128] for g in range(4)], n, [R_b512[1]], TRB)
                cp("scalar", o_cmT[:, :, c0:c0 + n], v, [pr], [R_ocmT])
                pq, prq = bank(PROJ)
                B.op("tensor", [(lambda e, dc=dc: e.matmul(pq[0:n, :], lhsT=hT_own[:, dc, c0:c0 + n], rhs=wqm[:, dc, :],
                                                           start=(dc == 0), stop=(dc == 15))) for dc in range(16)], [R_hTo, R_wqm], [prq])
                tq, rtq = t512[3], R_t512[3]
                B.op("scalar", lambda e: e.activation(out=tq[0:n, :], in_=pq[0:n, :], func=AF.Square), [prq], [rtq])
                B.op("vector", lambda e: e.tensor_reduce(out=ss[0:n, 0:4], in_=tq[0:n, :].rearrange("p (g d) -> p g d", d=128), axis=AX.X, op=ALU.add),
                     [rtq], [R["ss"]])
                rstd_from_ss(n, 4, 128, scale=(0, 4, 128 ** -0.5))
                B.op("vector", lambda e: e.tensor_tensor(out=tq[0:n, :].rearrange("p (g d) -> p g d", d=128),
                                                         in0=pq[0:n, :].rearrange("p (g d) -> p g d", d=128),
                                                         in1=ss2[0:n, 0:4].unsqueeze(2).to_broadcast([n, 4, 128]), op=ALU.mult), [prq, R["ss2"], rtq], [rtq])
                B.op(POOL_TT, lambda e: e.tensor_tensor(out=b512[2][0:n, :].rearrange("p (g d) -> p g d", d=128),
                                                         in0=tq[0:n, :].rearrange("p (g d) -> p g d", d=128),
                                                         in1=sv[0:n, MQG:MQG + 128].unsqueeze(1).to_broadcast([n, 4, 128]), op=ALU.mult),
                     [rtq, R["sv"]], [R_b512[2]])
                v, pr = tr_bf([b512[2][0:n, g * 128:(g + 1) * 128] for g in range(4)], n, [R_b512[2]], TRB)
                cp("vector", qmT[:, :, 0:n], v, [pr], [R_qmT])
                for g in range(4):
                    st, sr = bank(SCB)
                    B.op("tensor", [(lambda e, mb=mb: e.matmul(st[:, mb * 128:mb * 128 + n], lhsT=mkT[sm][:, g, mb * 128:(mb + 1) * 128],
                                                               rhs=qmT[:, g, 0:n], start=True, stop=True)) for mb in range(2)],
                         [R_mkT[sm], R_qmT], [sr])
                    P_, rp = PM[pmc[0] % 2], R_PM[pmc[0] % 2]
                    pmc[0] += 1
                    B.op("scalar", lambda e: e.activation(out=P_[:, :, 0:n], in_=st[:, 0:256].rearrange("p (m q) -> p m q", m=2)[:, :, 0:n], func=AF.Exp),
                         [sr], [rp])
                    ot, orr = bank([6, 7])
                    B.op("tensor", [(lambda e, mb=mb: e.matmul(ot[0:n, 0:129], lhsT=P_[:, mb, 0:n], rhs=mvb[sm][:, mb, g, 0:129],
                                                               start=(mb == 0), stop=(mb == 1))) for mb in range(2)], [rp, R_mvb[sm]], [orr])
                    B.op("vector", lambda e: e.reciprocal(out=ss2[0:n, 4:5], in_=ot[0:n, 128:129]), [orr], [R["ss2"]])
                    B.op("vector", lambda e, g=g: e.tensor_scalar(out=b512[1][0:n, g * 128:(g + 1) * 128], in0=ot[0:n, 0:128], scalar1=ss2[0:n, 4:5],
                                                                  scalar2=None, op0=ALU.mult), [orr, R["ss2"]], [R_b512[1]])
                v, pr = tr_bf([b512[1][0:n, g * 128:(g + 1) * 128] for g in range(4)], n, [R_b512[1]], TRB)
                cp("scalar", o_memT[:, :, c0:c0 + n], v, [pr], [R_omemT])
            B.barrier()

        if debug:
            B.dma(dbg_br[:, 0:8, :], o_daT[:], reads=[R_odaT], q="scalar")
            B.dma(dbg_br[:, 8:12, :], o_cmT[:], reads=[R_ocmT], q="scalar")
            B.dma(dbg_br[:, 12:16, :], o_memT[:], reads=[R_omemT], q="scalar")
        if stop_after <= 2:
            raise _Stop()
        PR3 = ExitStack()
        mergedT = B.sb("mergedT", [128, 16, OWN], BF16, PR3, side="right")
        R_mT3 = B.res("mergedT")
        with ExitStack() as s3:
            bg = B.sb("bg", [128, 48], F32, s3)
            R_bg = B.res()
            B.dma(bg[:], bgT[:, :], writes=[R_bg])
            NW = 2
            wg = [[B.sb("wg%d_%d" % (i, b), [128, 16, 128], BF16, s3) for b in range(3)] for i in range(NW)]
            wb = [B.sb("wb%d" % i, [128, 16, 128], BF16, s3) for i in range(NW)]
            R_wg = [[B.res() for b in range(3)] for i in range(NW)]
            R_wb = [B.res() for i in range(NW)]
            G = [B.sb("G%d" % b, [128, OWN], BF16, s3) for b in range(3)]
            R_G = [B.res() for b in range(3)]
            macc = B.sb("macc", [128, OWN], F32, s3)
            mtmp = B.sb("mtmp", [128, OWN], F32, s3)
            R_macc, R_mtmp = B.res(), B.res()
            GB, BB = [0, 1, 2, 3], [4, 5, 6, 7]
            for fc in range(16):
                wi = fc % NW
                for b in range(3):
                    wload(w_in_b[36 + b * 16 + fc], wg[wi][b][:], R_wg[wi][b])
                wload(w_br_b[fc], wb[wi][:], R_wb[wi])
                for b in range(3):
                    for (t0, tn) in TSL:
                        pg, prg = bank(GB)
                        B.op("tensor", [(lambda e, dc=dc: e.matmul(pg[:, 0:tn], lhsT=wg[wi][b][:, dc, :], rhs=hT_own[:, dc, t0:t0 + tn],
                                                                   start=(dc == 0), stop=(dc == 15))) for dc in range(16)], [R_wg[wi][b], R_hTo], [prg])
                        B.op("scalar", lambda e: e.activation(out=G[b][:, t0:t0 + tn], in_=pg[:, 0:tn], func=AF.Sigmoid,
                                                              bias=bg[:, b * 16 + fc:b * 16 + fc + 1]), [prg, R_bg], [R_G[b]])
                srcs = [(o_daT, R_odaT, 0, 8), (o_cmT, R_ocmT, 8, 4), (o_memT, R_omemT, 12, 4)]
                for b, (src, rsrc, k0, nk) in enumerate(srcs):
                    for (t0, tn) in TSL:
                        pb_, prb = bank(BB)
                        B.op("tensor", [(lambda e, kc=kc: e.matmul(pb_[:, 0:tn], lhsT=wb[wi][:, k0 + kc, :], rhs=src[:, kc, t0:t0 + tn],
                                                                   start=(kc == 0), stop=(kc == nk - 1))) for kc in range(nk)], [R_wb[wi], rsrc], [prb])
                        if b == 0:
                            B.op("vector", lambda e: e.tensor_tensor(out=macc[:, t0:t0 + tn], in0=pb_[:, 0:tn], in1=G[0][:, t0:t0 + tn], op=ALU.mult),
                                 [prb, R_G[0]], [R_macc])
                        else:
                            B.op("vector", lambda e: e.tensor_tensor(out=mtmp[:, t0:t0 + tn], in0=pb_[:, 0:tn], in1=G[b][:, t0:t0 + tn], op=ALU.mult),
                                 [prb, R_G[b]], [R_mtmp])
                            if b == 1:
                                B.op(POOL_TT, lambda e: e.tensor_tensor(out=macc[:, t0:t0 + tn], in0=macc[:, t0:t0 + tn], in1=mtmp[:, t0:t0 + tn],
                                                                         op=ALU.add), [R_macc, R_mtmp], [R_macc])
                            else:
                                B.op(POOL_TT, lambda e: e.tensor_tensor(out=mergedT[:, fc, t0:t0 + tn], in0=macc[:, t0:t0 + tn],
                                                                         in1=mtmp[:, t0:t0 + tn], op=ALU.add), [R_macc, R_mtmp], [R_mT3])
            B.barrier()
        P03.close()

        if debug:
            B.dma(dbg_mg[:, :, :], mergedT[:], reads=[R_mT3], q="scalar")
        if stop_after <= 3:
            raise _Stop()
        P46 = ExitStack()
        acc = B.sb("acc", [128, 9, 2048], F32, P46)
        R_acc = [B.res("acc%d" % i) for i in range(9)]
        with ExitStack() as s4:
            wo = [B.sb("wo%d" % i, [128, 16, 512], BF16, s4) for i in range(2)]
            R_wo = [B.res(), B.res()]
            xr = [B.sb("xr%d" % i, [128, 512], F32, s4) for i in range(3)]
            R_xr = [B.res() for i in range(3)]
            xc = [0]
            for cg in range(4):
                w_, rw = wo[cg % 2], R_wo[cg % 2]
                for j in range(4):
                    wload(w_out_b[cg * 4 + j], w_[:, :, j * 128:(j + 1) * 128], rw)
                for bi, (c0, n) in enumerate(TB_OWN):
                    po, pro = bank([0, 1, 2, 3])
                    B.op("tensor", [(lambda e, fc=fc: e.matmul(po[0:n, :], lhsT=mergedT[:, fc, c0:c0 + n], rhs=w_[:, fc, :],
                                                               start=(fc == 0), stop=(fc == 15))) for fc in range(16)], [R_mT3, rw], [pro])
                    x_, rx = xr[xc[0] % 3], R_xr[xc[0] % 3]
                    xc[0] += 1
                    B.dma(x_[0:n, :], xa[c0:c0 + n, cg * 512:(cg + 1) * 512], writes=[rx])
                    B.op("vector", lambda e: e.tensor_tensor(out=acc[0:n, bi, cg * 512:(cg + 1) * 512], in0=po[0:n, :], in1=x_[0:n, :], op=ALU.add),
                         [pro, rx], [R_acc[bi]])
            B.barrier()
        PR3.close()
        PR6 = ExitStack()
        xnT = B.sb("xnT", [128, 16, OWN], BF16, PR6, side="right")
        R_xnT = B.res("xnT")
        with ExitStack() as s4:
            gffn = B.sb("gffn", [128, 2048], F32, s4)
            R_gffn = B.res()
            B.dma(gffn[:], vecs[:, 4096:6144], writes=[R_gffn])
            xnb = [B.sb("xnb%d" % i, [128, 2048], BF16, s4) for i in range(2)]
            R_xnb = [B.res(), B.res()]
            for bi, (c0, n) in enumerate(TB_OWN):
                xb_, rxb = xnb[bi % 2], R_xnb[bi % 2]
                rms_rows(acc[0:n, bi, :], n, 2048, gffn[0:n, :], R_gffn, xb_[0:n, :], [R_acc[bi]], [rxb])
                for half in range(2):
                    v, pr = tr_bf([xb_[0:n, (half * 8 + j) * 128:(half * 8 + j + 1) * 128] for j in range(8)], n, [rxb], [4, 5, 6, 7])
                    cp("scalar" if half == 0 else "vector", xnT[:, half * 8:half * 8 + 8, c0:c0 + n], v, [pr], [R_xnT])
                B.dma(x1d[c0:c0 + n, :], acc[0:n, bi, :], reads=[R_acc[bi]], writes=[RX1D], q="scalar", semres=R_acc[bi])
                if debug:
                    B.dma(x1_o[c0:c0 + n, :], acc[0:n, bi, :], reads=[R_acc[bi]], q="scalar")
            B.barrier()
        P46.close()

        if stop_after <= 4:
            raise _Stop()
        with ExitStack() as s5:
            utb = [B.sb("utb", [128, 16, 128], BF16, s5) for _ in range(2)]
            R_utb = [B.res(), B.res()]
            gab = [B.sb("gab", [128, OWN], BF16, s5) for _ in range(2)]
            R_gab = [B.res(), B.res()]
            AB = [1, 4, 5, 6, 7]
            a_u = [0]
            a_prev = [None]

            def a_unit():
                u = a_u[0]
                if u >= 384:
                    if a_prev[0] is not None:
                        a_prev[0]()
                        a_prev[0] = None
                    return
                a_u[0] += 1
                c, ti = divmod(u, 3)
                t0, tn = TSL[ti]
                if u == 0:
                    wload(peer_ut[0], utb[0][:], R_utb[0], eng="gpsimd")
                if ti == 0 and c + 1 < 128:
                    wload(peer_ut[c + 1], utb[(c + 1) % 2][:], R_utb[(c + 1) % 2], eng="gpsimd")
                u_, ru = utb[c % 2], R_utb[c % 2]
                pa_, pra = bank(AB)
                B.op("tensor", [(lambda e, dc=dc: e.matmul(pa_[:, 0:tn], lhsT=u_[:, dc, :], rhs=xnT[:, dc, t0:t0 + tn],
                                                           start=(dc == 0), stop=(dc == 15))) for dc in range(16)], [ru, R_xnT], [pra])
                if a_prev[0] is not None:
                    a_prev[0]()

                def gel():
                    g_, rg = gab[c % 2], R_gab[c % 2]
                    B.op("scalar", lambda e: e.activation(out=g_[:, t0:t0 + tn], in_=pa_[:, 0:tn], func=AF.Gelu), [pra], [rg])
                    if ti == 2:
                        B.dma(gad[c, :, :], g_[:, :], reads=[rg], writes=[RGAD], q="scalar", semres=rg)
                a_prev[0] = gel

            pkb = B.sb("pkb", [128, 16, 128], BF16, s5)
            R_pkb = B.res()
            wload(pk_t[:, :, :], pkb[:], R_pkb)
            v16 = B.sb("v16", [128, 9, 16, 16], F32, s5)
            x16 = B.sb("x16", [128, 9, 16, 16], U32, s5)
            R_v16 = [B.res() for i in range(9)]
            R_x16 = [B.res() for i in range(9)]
            with ExitStack() as s5a:
                wpq = [B.sb("wpq%d" % i, [128, 16, 128], BF16, s5a) for i in range(2)]
                R_wpq = [B.res(), B.res()]
                qTs = [B.sb("qTs%d" % i, [128, OWN], BF16, s5a) for i in range(2)]
                R_qTs = [B.res(), B.res()]
                s1b = [B.sb("s1b%d" % i, [128, 128], F32, s5a) for i in range(3)]
                s2b = [B.sb("s2b%d" % i, [128, 128], F32, s5a) for i in range(3)]
                R_s1b = [B.res() for _ in range(3)]
                R_s2b = [B.res() for _ in range(3)]
                p5 = Pipe()
                sc_ = [0]
                for hp in range(16):
                    w_, rw = wpq[hp % 2], R_wpq[hp % 2]
                    wload(w_pq_b[hp], w_[:], rw)
                    q_, rq = qTs[hp % 2], R_qTs[hp % 2]
                    for ti, (t0, tn) in enumerate(TSL):
                        pq, prq = bank([0])
                        B.op("tensor", [(lambda e, dc=dc: e.matmul(pq[:, 0:tn], lhsT=w_[:, dc, :], rhs=xnT[:, dc, t0:t0 + tn],
                                                                   start=(dc == 0), stop=(dc == 15))) for dc in range(16)], [rw, R_xnT], [prq])
                        cp("scalar", q_[:, t0:t0 + tn], pq[:, 0:tn], [prq], [rq])
                    for bi, (c0, n) in enumerate(TB_OWN):
                        def mk5(bi=bi, c0=c0, n=n, hp=hp, q_=q_, rq=rq):
                            a_, ra = s1b[sc_[0] % 3], R_s1b[sc_[0] % 3]
                            b_, rb = s2b[sc_[0] % 3], R_s2b[sc_[0] % 3]
                            sc_[0] += 1

                            def Sa():
                                st, sr = bank([2, 3])
                                B.op("tensor", lambda e: e.matmul(st[0:n, 0:128], lhsT=q_[:, c0:c0 + n], rhs=pkb[:, hp, :], start=True, stop=True),
                                     [rq, R_pkb], [sr])
                                cp("scalar", a_[0:n, :], st[0:n, 0:128], [sr], [ra])

                            def Sb():
                                B.op("vector", lambda e: e.max(out=v16[0:n, bi, hp, 0:8], in_=a_[0:n, :]), [ra], [R_v16[bi]])
                                B.op("vector", lambda e: e.max_index(out=x16[0:n, bi, hp, 0:8], in_max=v16[0:n, bi, hp, 0:8], in_values=a_[0:n, :]),
                                     [ra, R_v16[bi]], [R_x16[bi]])
                                B.op("vector", lambda e: e.match_replace(out=b_[0:n, :], in_to_replace=v16[0:n, bi, hp, 0:8], in_values=a_[0:n, :],
                                                                         imm_value=-1e30), [ra, R_v16[bi]], [rb])
                                B.op("vector", lambda e: e.max(out=v16[0:n, bi, hp, 8:16], in_=b_[0:n, :]), [rb], [R_v16[bi]])
                                B.op("vector", lambda e: e.max_index(out=x16[0:n, bi, hp, 8:16], in_max=v16[0:n, bi, hp, 8:16], in_values=b_[0:n, :]),
                                     [rb, R_v16[bi]], [R_x16[bi]])
                            return [Sa, Sb]
                        p5.push(mk5())
                        if (hp * 9 + bi) % 2 == 0:
                            a_unit()
                p5.flush()
                B.barrier()
            x16f = B.sb("x16f", [128, 16, 16], F32, s5)
            cand = B.sb("cand", [128, 8, 256], F32, s5)
            c16 = B.sb("c16", [128, 8, 16], F32, s5)
            p16 = B.sb("p16", [128, 8, 16], U32, s5)
            pa = B.sb("pa", [128, 2, 128], U32, s5)
            paf = B.sb("paf", [128, 2, 8, 16], F32, s5)
            eq = B.sb("eq", [128, 8, 16, 16], BF16, s5)
            trio = B.sb("trio", [128, 3, 128], F32, s5)
            ce = B.sb("ce", [128, 8, 16], F32, s5)
            Q1 = [B.sb("Q1", [128, 64, 128], BF16, s5) for _ in range(2)]
            Q2 = [B.sb("Q2", [128, 64, 128], BF16, s5) for _ in range(2)]
            R_Q1, R_Q2 = [B.res(), B.res()], [B.res(), B.res()]
            iota128b = B.sb("iota128b", [128, 128], BF16, s5)
            T3b = B.sb("T3b", [128, 3, 128], BF16, s5)
            Wsb = [B.sb("Wsb%d" % i, [128, 128, 64], BF16, s5) for i in range(2)]
            iota16 = B.sb("iota16", [128, 16], F32, s5)
            Rn = {n: B.res(n) for n in ["x16f", "cand", "c16", "p16", "pa", "paf", "eq", "trio", "ce", "T3", "Q1", "Q2", "iota16"]}
            R_Wsb = [[B.res() for _ in range(16)] for _ in range(2)]
            cp("vector", iota128b[:], iota128[:], [R["iota"]], [R["iota"]])
            cp("vector", iota16[:], iota128[:, 0:16], [R["iota"]], [Rn["iota16"]])
            wc = [0]
            for bi, (c0, n) in enumerate(TB_OWN):
                cp("vector", x16f[0:n].rearrange("p a b -> p (a b)"), x16[0:n, bi].rearrange("p a b -> p (a b)"), [R_x16[bi]], [Rn["x16f"]])
                vv = v16[0:n, bi].rearrange("p (h t) a -> p h t a", t=2)
                B.op("vector", lambda e: e.tensor_tensor(out=cand[0:n].rearrange("p h (a b) -> p h a b", a=16),
                                                         in0=vv[:, :, 0, :].unsqueeze(3).to_broadcast([n, 8, 16, 16]),
                                                         in1=vv[:, :, 1, :].unsqueeze(2).to_broadcast([n, 8, 16, 16]), op=ALU.add),
                     [R_v16[bi]], [Rn["cand"]])
                for h in range(8):
                    B.op("vector", lambda e, h=h: e.max(out=c16[0:n, h, 0:8], in_=cand[0:n, h, :]), [Rn["cand"]], [Rn["c16"]])
                    a_unit()
                    B.op("vector", lambda e, h=h: e.max_index(out=p16[0:n, h, 0:8], in_max=c16[0:n, h, 0:8], in_values=cand[0:n, h, :]),
                         [Rn["cand"], Rn["c16"]], [Rn["p16"]])
                    B.op("vector", lambda e, h=h: e.match_replace(out=cand[0:n, h, :], in_to_replace=c16[0:n, h, 0:8], in_values=cand[0:n, h, :],
                                                                  imm_value=-1e30), [Rn["cand"], Rn["c16"], Rn["p16"]], [Rn["cand"]])
                    a_unit()
                    B.op("vector", lambda e, h=h: e.max(out=c16[0:n, h, 8:16], in_=cand[0:n, h, :]), [Rn["cand"]], [Rn["c16"]])
                    B.op("vector", lambda e, h=h: e.max_index(out=p16[0:n, h, 8:16], in_max=c16[0:n, h, 8:16], in_values=cand[0:n, h, :]),
                         [Rn["cand"], Rn["c16"]], [Rn["p16"]])
                    a_unit()
                p16f = p16[0:n].rearrange("p h k -> p (h k)")
                B.op("vector", lambda e: e.tensor_single_scalar(out=pa[0:n, 0, :], in_=p16f, scalar=4, op=ALU.logical_shift_right), [Rn["p16"]], [Rn["pa"]])
                B.op("vector", lambda e: e.tensor_single_scalar(out=pa[0:n, 1, :], in_=p16f, scalar=15, op=ALU.bitwise_and), [Rn["p16"]], [Rn["pa"]])
                cp("vector", paf[0:n].rearrange("p t h k -> p (t h k)"), pa[0:n].rearrange("p t k -> p (t k)"), [Rn["pa"]], [Rn["paf"]])
                xf = x16f[0:n].rearrange("p (h t) a -> p h t a", t=2)
                for t in range(2):
                    B.op("vector", lambda e, t=t: e.tensor_tensor(out=eq[0:n], in0=iota16[0:n, :].unsqueeze(1).unsqueeze(1).to_broadcast([n, 8, 16, 16]),
                                                                  in1=paf[0:n, t].unsqueeze(3).to_broadcast([n, 8, 16, 16]), op=ALU.is_equal),
                         [Rn["iota16"], Rn["paf"]], [Rn["eq"]])
                    B.op("vector", lambda e, t=t: e.tensor_tensor(out=eq[0:n], in0=eq[0:n], in1=xf[:, :, t, :].unsqueeze(2).to_broadcast([n, 8, 16, 16]),
                                                                  op=ALU.mult), [Rn["eq"], Rn["x16f"]], [Rn["eq"]])
                    B.op("vector", lambda e, t=t: e.tensor_reduce(out=trio[0:n, t, :].rearrange("p (h k) -> p h k", h=8), in_=eq[0:n], axis=AX.X, op=ALU.add),
                         [Rn["eq"]], [Rn["trio"]])
                B.op("vector", lambda e: e.tensor_tensor(out=ce[0:n], in0=c16[0:n], in1=c16[0:n, :, 0:1].to_broadcast([n, 8, 16]), op=ALU.subtract),
                     [Rn["c16"]], [Rn["ce"]])
                B.op("scalar", lambda e: e.activation(out=ce[0:n], in_=ce[0:n], func=AF.Exp), [Rn["ce"]], [Rn["ce"]])
                B.op("vector", lambda e: e.tensor_reduce(out=ss[0:n, 0:8], in_=ce[0:n], axis=AX.X, op=ALU.add), [Rn["ce"]], [R["ss"]])
                B.op("vector", lambda e: e.reciprocal(out=ss2[0:n, 0:8], in_=ss[0:n, 0:8]), [R["ss"]], [R["ss2"]])
                B.op("vector", lambda e: e.tensor_tensor(out=trio[0:n, 2, :].rearrange("p (h k) -> p h k", h=8), in0=ce[0:n],
                                                         in1=ss2[0:n, 0:8].unsqueeze(2).to_broadcast([n, 8, 16]), op=ALU.mult),
                     [Rn["ce"], R["ss2"]], [Rn["trio"]])
                pt, pr = bank([0])
                B.op("tensor", [(lambda e, j=j: e.transpose(pt[:, j * 128:j * 128 + n], trio[0:n, j, :], ident_f[0:n, 0:n])) for j in range(3)],
                     [Rn["trio"], R["ident"]], [pr])
                cp("scalar", T3b[:, :, 0:n], pt[:, 0:384].rearrange("p (j t) -> p j t", j=3)[:, :, 0:n], [pr], [Rn["T3"]])
                for hf in range((n + 63) // 64):
                    t0 = hf * 64
                    qi = wc[0] % 2
                    Q1_, Q2_, rQ1, rQ2 = Q1[qi], Q2[qi], R_Q1[qi], R_Q2[qi]
                    W_, rWs = Wsb[qi], R_Wsb[qi]
                    wc[0] += 1
                    B.op("vector", lambda e: e.tensor_tensor(out=Q1_[:], in0=iota128b[:, :].unsqueeze(1).to_broadcast([128, 64, 128]),
                                                             in1=T3b[:, 0, t0:t0 + 64].unsqueeze(2).to_broadcast([128, 64, 128]), op=ALU.is_equal),
                         [R["iota"], Rn["T3"]], [rQ1])
                    B.op("vector", lambda e: e.tensor_tensor(out=Q1_[:], in0=Q1_[:], in1=T3b[:, 2, t0:t0 + 64].unsqueeze(2).to_broadcast([128, 64, 128]),
                                                             op=ALU.mult), [rQ1, Rn["T3"]], [rQ1])
                    B.op("vector", lambda e: e.tensor_tensor(out=Q2_[:], in0=iota128b[:, :].unsqueeze(1).to_broadcast([128, 64, 128]),
                                                             in1=T3b[:, 1, t0:t0 + 64].unsqueeze(2).to_broadcast([128, 64, 128]), op=ALU.is_equal),
                         [R["iota"], Rn["T3"]], [rQ2])
                    for t4 in range(16):
                        pw, prw = bank([2, 3])
                        pwv = pw[:, :].rearrange("r (c t) -> r t c", t=4)
                        B.op("tensor", [(lambda e, tt=tt: e.matmul(pwv[:, tt, :], lhsT=Q2_[:, t4 * 4 + tt, :], rhs=Q1_[:, t4 * 4 + tt, :],
                                                                   start=True, stop=True)) for tt in range(4)], [rQ1, rQ2], [prw])
                        cp("scalar", W_[:, :, t4 * 4:t4 * 4 + 4],
                           pw[:, :].rearrange("r (c t) -> r c t", t=4), [prw], [rWs[t4]])
                        if t4 % 2 == 1:
                            a_unit()
                    B.dma(wd[:, :, c0 + t0:c0 + t0 + 64].rearrange("c r t -> r c t"), W_[:], reads=rWs, writes=[RWD], q="scalar", semres=rWs[0])
            while a_u[0] < 384 or a_prev[0] is not None:
                a_unit()
            B.barrier()
        WST.close()

        PR6.close()
        JS.close()
        if stop_after <= 5:
            raise _Stop()
        P6 = ExitStack()
        acc = B.sb("acc6", [128, 9, 2048], F32, P6)
        R_acc = [B.res("acc6_%d" % i) for i in range(9)]
        for bi, (c0, n) in enumerate(TB_OWN):
            B.dma(acc[0:n, bi, :], x1d[c0:c0 + n, :], reads=[RX1D], writes=[R_acc[bi]])
        GS = 8
        NG = 128 // GS
        with ExitStack() as s6:
            NSTG = 2
            stg = [B.sb("stg%d" % i, [128, 2048], F32, s6) for i in range(NSTG)]
            R_stg = [B.res() for _ in range(NSTG)]
            vbf = [B.sb("vbf%d" % i, [128, GS, 2048], BF16, s6) for i in range(2)]
            R_vbf = [[B.res() for ci in range(GS)] for i in range(2)]
            WA = [B.sb("WA%d" % i, [128, GS, OWN], BF16, s6) for i in range(2)]
            R_WA = [[B.res() for ci in range(GS)] for i in range(2)]
            NWC = GS
            wcb = [B.sb("wcb%d" % i, [128, OWN], BF16, s6) for i in range(NWC)]
            R_wcb = [B.res() for i in range(NWC)]
            lc = [0]

            def sload(src, dst, dres, eng):
                i = lc[0] % NSTG
                lc[0] += 1
                st = stg[i][:]
                if len(src.shape) == 3:
                    st = st.rearrange("p (a b) -> p a b", a=src.shape[1])
                B.dma(st, src, writes=[R_stg[i]])
                cp(eng, dst, st, [R_stg[i]], [dres])

            def stage_A(g):
                gi = g % 2
                B.dma(WA[gi][:, :, :], gad[g * GS:(g + 1) * GS, :, :].rearrange("c r t -> r c t"), reads=[RGAD], writes=R_WA[gi],
                      semres=R_WA[gi][0])
                mults = []
                for ci in range(GS):
                    c = g * GS + ci
                    sload(peer_v[c * 128:(c + 1) * 128, :], vbf[gi][:, ci, :], R_vbf[gi][ci], "gpsimd" if c % 4 == 0 else "scalar")
                    wc_, rwc = wcb[c % NWC], R_wcb[c % NWC]
                    B.dma(wc_[:, :], wd[c, :, 0:OWN], reads=[RWD], writes=[rwc])

                    def mult(ci=ci, wc_=wc_, rwc=rwc):
                        B.op("vector", lambda e: e.tensor_tensor(out=WA[gi][:, ci, :], in0=WA[gi][:, ci, :], in1=wc_[:, :], op=ALU.mult),
                             [rwc, R_WA[gi][ci]], [R_WA[gi][ci]])
                    mults.append(mult)
                return mults

            def stage_O(g, pending):
                gi = g % 2
                k = 0
                for bi, (c0, n) in enumerate(TB_OWN):
                    for cg in range(4):
                        po, pro = bank([0, 1, 2, 3, 4, 5, 6, 7])
                        B.op("tensor", [(lambda e, ci=ci: e.matmul(po[0:n, :], lhsT=WA[gi][:, ci, c0:c0 + n], rhs=vbf[gi][:, ci, cg * 512:(cg + 1) * 512],
                                                                   start=(ci == 0), stop=(ci == GS - 1))) for ci in range(GS)],
                             R_WA[gi] + R_vbf[gi], [pro])
                        B.op("vector", lambda e: e.tensor_tensor(out=acc[0:n, bi, cg * 512:(cg + 1) * 512], in0=po[0:n, :],
                                                                 in1=acc[0:n, bi, cg * 512:(cg + 1) * 512], op=ALU.add), [pro, R_acc[bi]], [R_acc[bi]])
                        k += 1
                        if pending and k % 4 == 0:
                            pending.pop(0)()
                while pending:
                    pending.pop(0)()

            for m_ in stage_A(0):
                m_()
            for g in range(NG):
                pend = stage_A(g + 1) if g + 1 < NG else []
                stage_O(g, pend)
            for bi, (c0, n) in enumerate(TB_OWN):
                B.dma(y_o[c0:c0 + n, :], acc[0:n, bi, :], reads=[R_acc[bi]], q="scalar")
    except _Stop:
        pass
    B.emit()
    return nc


def _prep(inp):
    f32 = np.float32
    g = lambda k: np.asarray(inp[k], dtype=f32)
    w_in = g("w_in")[0]
    shared = {
        "w_in_b": np.ascontiguousarray(w_in.reshape(16, 128, 84, 128).transpose(2, 1, 0, 3)),
        "w_br_b": np.ascontiguousarray(g("w_branch")[0].reshape(16, 128, 16, 128).transpose(2, 1, 0, 3)),
        "w_out_b": np.ascontiguousarray(g("w_out")[0].reshape(16, 128, 16, 128).transpose(2, 1, 0, 3)),
        "w_pq_b": np.ascontiguousarray(g("peer_w_query")[0].reshape(16, 128, 16, 128).transpose(2, 1, 0, 3)),
        "w_mkv_b": np.ascontiguousarray(g("w_mem_kv")[0].reshape(16, 128, 8, 128).transpose(2, 1, 0, 3)),
        "peer_ut": np.ascontiguousarray(g("peer_u")[0].reshape(128, 128, 16, 128).transpose(0, 3, 2, 1)),
        "peer_v": np.ascontiguousarray(g("peer_v")[0]),
        "pk_t": np.ascontiguousarray(g("peer_sub_keys")[0].reshape(16, 128, 128).transpose(2, 0, 1)),
        "bgT": np.ascontiguousarray(g("b_gate")[0].reshape(48, 128).T),
        "bsT": np.ascontiguousarray(g("cm_bs")[0].T),
        "wsl": np.ascontiguousarray(g("cm_ws")[0].transpose(1, 0, 2)),
    }
    vec = np.concatenate([
        g("norm_mix_g")[0], g("norm_mem_g")[0], g("norm_ffn_g")[0],
        g("da_qn_g")[0], g("da_kn_g")[0], g("da_out_g")[0], g("cm_ln_g")[0], g("cm_ln_b")[0],
        g("mem_qn_g")[0], g("mem_kn_g")[0],
        g("da_lambda_q1")[0], g("da_lambda_k1")[0], g("da_lambda_q2")[0], g("da_lambda_k2")[0]])
    assert vec.shape[0] == 6144 + NSV
    shared["vecs"] = np.ascontiguousarray(np.tile(vec[None, :], (128, 1)))
    inv = (np.float32(500000.0) ** (-(np.arange(8, dtype=f32)) / np.float32(8))).astype(f32)
    xp, xs, mp = g("x_prompt"), g("x_sample"), g("mem_prompt")
    cdk, cdv, cmk, cmv = g("cache_da_k")[0], g("cache_da_v")[0], g("cache_mem_k")[0], g("cache_mem_v")[0]
    maps = []
    for c in range(8):
        b, par = c // 2, c % 2
        own = OWN_BLKS[par]
        oth = OWN_BLKS[1 - par]
        rows_a = np.concatenate([np.arange(t * 128, (t + 1) * 128) for t in own])
        rows_b = np.concatenate([np.arange(t * 128, (t + 1) * 128) for t in oth])
        xa = np.concatenate([xp[b][rows_a], xs[c], xp[b][rows_b]], axis=0)
        pos = np.concatenate([rows_a, 1024 + np.arange(64), rows_b]).astype(f32)
        ang = pos[:, None] * inv[None, :]
        cs_flat = np.concatenate([np.cos(ang), np.sin(ang)], axis=1).astype(f32)
        cs = np.zeros((128, 17, 16), f32)
        for bi in range(17):
            if bi < 8:
                r0, n = bi * 128, 128
            elif bi == 8:
                r0, n = 1024, 64
            else:
                r0, n = 1088 + (bi - 9) * 128, 128
            cs[0:n, bi, :] = cs_flat[r0:r0 + n]
        cmask = np.zeros((128, 8, 128), f32)
        for i in range(8):
            if oth[i] < own[i]:
                cmask[:, i, :] = 1.0
        m = dict(shared)
        m.update({
            "xa": np.ascontiguousarray(xa), "cs": cs, "cmask": cmask,
            "mem_x": np.ascontiguousarray(mp[b]),
            "c_dak": np.ascontiguousarray(cdk[c].reshape(1024, 1024)),
            "c_dav": np.ascontiguousarray(cdv[c].reshape(1024, 1024)),
            "c_mk": np.ascontiguousarray(cmk[c].reshape(256, 512)),
            "c_mv": np.ascontiguousarray(cmv[c].reshape(256, 512)),
        })
        maps.append(m)
    return maps


_LAST = {}


def kernel(**inputs):
    maps = _prep(inputs)
    nc = build_program(debug=DEBUG)
    res = run_bass_kernel_spmd(nc, maps, core_ids=list(range(8)))
    outs = res.results
    f32 = np.float32
    yp = np.zeros((4, 2048, 2048), f32)
    ys = np.zeros((8, 64, 2048), f32)
    kp = np.zeros((1, 4, 2048, 8, 128), f32)
    vp = np.zeros((1, 4, 2048, 8, 128), f32)
    mkp = np.zeros((1, 4, 256, 4, 128), f32)
    mvp = np.zeros((1, 4, 256, 4, 128), f32)
    ksn = np.zeros((1, 8, 64, 8, 128), f32)
    vsn = np.zeros((1, 8, 64, 8, 128), f32)
    cvs = np.zeros((1, 8, 64, 512), f32)
    for c in range(8):
        b, par = c // 2, c % 2
        o = outs[c]
        rows_a = np.concatenate([np.arange(t * 128, (t + 1) * 128) for t in OWN_BLKS[par]])
        yp[b][rows_a] = o["y"][0:1024]
        ys[c] = o["y"][1024:1088]
        kp[0, b][rows_a] = o["ok"][0:1024].reshape(1024, 8, 128)
        vp[0, b][rows_a] = o["ov"][0:1024].reshape(1024, 8, 128)
        ksn[0, c] = o["ok"][1024:1088].reshape(64, 8, 128)
        vsn[0, c] = o["ov"][1024:1088].reshape(64, 8, 128)
        cvs[0, c] = o["ocv"]
        if par == 0:
            mkp[0, b] = o["omk"].reshape(256, 4, 128)
            mvp[0, b] = o["omv"].reshape(256, 4, 128)
        if DEBUG:
            _LAST.setdefault("x1", {})[c] = o["x1dbg"]
    return (yp, ys, kp, vp, mkp, mvp, ksn, vsn, cvs)
```

```python
from contextlib import ExitStack
import numpy as np
import concourse.bass as bass
import concourse.mybir as mybir
from concourse.bass_utils import run_bass_kernel_spmd

F32 = mybir.dt.float32
BF16 = mybir.dt.bfloat16
I32 = mybir.dt.int32
U32 = mybir.dt.uint32
ALU = mybir.AluOpType
AF = mybir.ActivationFunctionType
AX = mybir.AxisListType

DEBUG = False
POOL_TT = "vector"
EPS = 1e-6
NA, NS, NB = 1024, 64, 1024
OWN = NA + NS
OWN_BLKS = {0: [0, 3, 4, 7, 8, 11, 12, 15], 1: [1, 2, 5, 6, 9, 10, 13, 14]}
GQ, GK, GOUT, LNG, LNB, MQG, MKG, LQ = 0, 64, 128, 256, 768, 1280, 1408, 1536
NSV = 1792


class _Stop(Exception):
    pass


class _RotBuf:
    def __init__(self, bufs):
        self.bufs = bufs
        self.i = 0

    def __getitem__(self, k):
        return self.bufs[self.i][k]


class Pipe:
    def __init__(self):
        self.q = []

    def push(self, stages):
        self.q.insert(0, [stages, 0])
        self._step()

    def _step(self):
        for item in list(self.q):
            item[0][item[1]]()
            item[1] += 1
        self.q = [it for it in self.q if it[1] < len(it[0])]

    def flush(self):
        while self.q:
            self._step()


class Res:
    __slots__ = ("name", "w", "r", "dsem", "dcnt")

    def __init__(self, name):
        self.name = name
        self.w = None
        self.r = {}
        self.dsem = None
        self.dcnt = 0


class Builder:
    ENGS = ("sync", "tensor", "vector", "scalar", "gpsimd")

    def __init__(self, nc):
        self.nc = nc
        self.ops = {e: [] for e in self.ENGS}
        self.sems = {e: nc.alloc_semaphore("s_" + e) for e in self.ENGS}
        self.cnt = {e: 0 for e in self.ENGS}
        self.known = {e: {} for e in self.ENGS}
        self.dres = []
        self.nres = 0
        self.nins = 0
        self.arena = None
        self.nops = 0
        self.stop_ops = -1
        self.lp = self.rp = 0
        self.peak = 0

    def sb(self, name, shape, dtype, stack=None, side=None):
        if self.arena is None:
            total = (self.nc.sbuf_bytes_remaining - 4096) // 64 * 64
            total = min(total, 218 * 1024)
            self.arena = self.nc.alloc_sbuf_tensor("arena", [128, total // 2], BF16)
            self.lp, self.rp = 0, total
        n = 1
        for d in shape[1:]:
            n *= d
        esz = 2 if dtype == BF16 else 4
        nbytes = (n * esz + 63) // 64 * 64
        if side == "right":
            self.rp -= nbytes
            off = self.rp
            if stack is not None:
                stack.callback(self._restore, "rp", off + nbytes)
        else:
            off = self.lp
            self.lp += nbytes
            if stack is not None:
                stack.callback(self._restore, "lp", off)
        assert self.lp <= self.rp, ("SBUF arena exhausted", name, self.lp, self.rp)
        self.peak = max(self.peak, self.lp + (self.arena.shape[1] * 2 - self.rp))
        ap = self.arena[0:shape[0], off // 2:(off + n * esz) // 2]
        if dtype != BF16:
            ap = ap.bitcast(dtype)
        if len(shape) == 3:
            ap = ap.rearrange("p (a b) -> p a b", a=shape[1], b=shape[2])
        elif len(shape) == 4:
            ap = ap.rearrange("p (a b c) -> p a b c", a=shape[1], b=shape[2], c=shape[3])
        return ap

    def _restore(self, which, val):
        setattr(self, which, val)

    def res(self, name=None):
        self.nres += 1
        return Res(name or ("r%d" % self.nres))

    def _wait(self, eng, toks):
        k = self.known[eng]
        own = self.sems[eng] if eng == "tensor" else None
        for (sem, val) in toks:
            if sem is own:
                continue
            if k.get(sem, 0) < val:
                k[sem] = val
                self.ops[eng].append(("wait", sem, val))

    def _deps(self, eng, reads, writes):
        toks = []
        for r in reads:
            if r.w is not None:
                toks.append(r.w)
        for w in writes:
            if w.w is not None:
                toks.append(w.w)
            toks.extend(w.r.items())
        self._wait(eng, toks)

    def _flush(self):
        for ename in self.ENGS:
            lst = self.ops[ename]
            if not lst:
                continue
            e = getattr(self.nc, ename)
            sem = self.sems[ename]
            for o in lst:
                if o[0] == "wait":
                    e.wait_ge(o[1], o[2])
                elif o[0] == "ins":
                    ins = o[1](e)
                    if o[2]:
                        ins.then_inc(sem, 1)
                else:
                    e.dma_start(out=o[1], in_=o[2]).then_inc(o[3], 16)
                self.nins += 1
            self.ops[ename] = []

    def _update(self, tok, reads, writes):
        for r in reads:
            if r.r.get(tok[0], 0) < tok[1]:
                r.r[tok[0]] = tok[1]
        for w in writes:
            w.w = tok
            w.r = {}

    def op(self, eng, fns, reads=(), writes=()):
        if not isinstance(fns, (list, tuple)):
            fns = [fns]
        self.nops += 1
        if self.nops == self.stop_ops:
            raise _Stop()
        self._deps(eng, reads, writes)
        s = self.sems[eng]
        self.cnt[eng] += 1
        tok = (s, self.cnt[eng])
        n = len(fns)
        for i, fn in enumerate(fns):
            self.ops[eng].append(("ins", fn, i == n - 1))
        self._update(tok, reads, writes)
        self._flush()
        return tok

    def dma(self, out_ap, in_ap, reads=(), writes=(), q="sync", semres=None):
        res = semres if semres is not None else (writes[0] if writes else reads[0])
        if res.dsem is None:
            res.dsem = self.nc.alloc_semaphore("d%d" % len(self.dres))
            self.dres.append(res)
        self._deps(q, reads, writes)
        if res.dcnt > 0:
            self._wait(q, [(res.dsem, res.dcnt * 16)])
        res.dcnt += 1
        tok = (res.dsem, res.dcnt * 16)
        self.ops[q].append(("dma", out_ap, in_ap, res.dsem))
        self._update(tok, reads, writes)
        self._flush()
        return tok

    def barrier(self):
        toks = [(self.sems[e], self.cnt[e]) for e in self.ENGS if self.cnt[e] > 0]
        toks += [(r.dsem, r.dcnt * 16) for r in self.dres]
        for e in self.ENGS:
            own = self.sems[e]
            k = self.known[e]
            for (sem, val) in toks:
                if k.get(sem, 0) < val:
                    k[sem] = val
                    self.ops[e].append(("wait", sem, val))
        self._flush()

    def emit(self):
        self.barrier()
        self._flush()


def build_program(debug=False, stop_after=99, stop_ops=-1):
    nc = bass.Bass("TRN2", target_bir_lowering=False, dynamic_dma_scratch_size=256)
    B = Builder(nc)
    B.stop_ops = stop_ops

    def din(name, shape, dt=F32):
        return nc.dram_tensor(name, list(shape), dt, kind="ExternalInput").ap()

    def dout(name, shape, dt=F32):
        return nc.dram_tensor(name, list(shape), dt, kind="ExternalOutput").ap()

    xa = din("xa", [OWN + NB, 2048])
    cs_d = din("cs", [128, 17, 16])
    cmask_d = din("cmask", [128, 8, 128])
    mem_x = din("mem_x", [256, 2048])
    c_dak = din("c_dak", [1024, 1024])
    c_dav = din("c_dav", [1024, 1024])
    c_mk = din("c_mk", [256, 512])
    c_mv = din("c_mv", [256, 512])
    w_in_b = din("w_in_b", [84, 128, 16, 128])
    w_br_b = din("w_br_b", [16, 128, 16, 128])
    w_out_b = din("w_out_b", [16, 128, 16, 128])
    w_pq_b = din("w_pq_b", [16, 128, 16, 128])
    w_mkv_b = din("w_mkv_b", [8, 128, 16, 128])
    peer_ut = din("peer_ut", [128, 128, 16, 128])
    peer_v = din("peer_v", [16384, 2048])
    pk_t = din("pk_t", [128, 16, 128])
    vecs = din("vecs", [128, 6144 + NSV])
    bgT = din("bgT", [128, 48])
    bsT = din("bsT", [128, 4])
    wsl = din("wsl", [128, 4, 128])

    y_o = dout("y", [OWN, 2048])
    ok_o = dout("ok", [OWN, 1024])
    ov_o = dout("ov", [OWN, 1024])
    omk_o = dout("omk", [256, 512])
    omv_o = dout("omv", [256, 512])
    ocv_o = dout("ocv", [64, 512])
    if debug:
        x1_o = dout("x1dbg", [OWN, 2048])
        dbg_br = dout("dbg_br", [128, 16, OWN], BF16)
        dbg_mg = dout("dbg_mg", [128, 16, OWN], BF16)
    x1d = nc.dram_tensor("x1d", [OWN, 2048], F32, kind="Internal").ap()
    wd = nc.dram_tensor("wd", [128, 128, 1152], BF16, kind="Internal").ap()
    gad = nc.dram_tensor("gad", [128, 128, OWN], BF16, kind="Internal").ap()
    RGAD = B.res("gad")
    RX1D = B.res("x1d")
    RWD = B.res("wd")

    ps = [nc.alloc_psum_tensor("ps%d" % i, [128, 512], F32) for i in range(8)]
    PR = [B.res("ps%d" % i) for i in range(8)]
    rot = {}

    def bank(ids):
        k = tuple(ids)
        i = rot.get(k, 0)
        rot[k] = i + 1
        b = ids[i % len(ids)]
        return ps[b], PR[b]

    def cp(eng, out, in_, reads, writes):
        if eng == "scalar":
            return B.op(eng, lambda e: e.copy(out=out, in_=in_), reads, writes)
        return B.op(eng, lambda e: e.tensor_copy(out=out, in_=in_), reads, writes)

    TBA = [(i * 128, 128) for i in range(8)]
    TBS = [(1024, 64)]
    TBB = [(1088 + i * 128, 128) for i in range(8)]
    TB_OWN = TBA + TBS
    TB_ALL = TBA + TBS + TBB
    TSL = [(0, 512), (512, 512), (1024, 64)]

    C = ExitStack()
    ident_bf = B.sb("ident_bf", [128, 128], BF16)
    ident_f = B.sb("ident_f", [128, 128], F32)
    iota128 = B.sb("iota128", [128, 128], F32)
    sv = B.sb("sv", [128, NSV], F32)
    neglam = B.sb("neglam", [128, 1], F32)
    gkq = B.sb("gkq", [128, 4, 64], F32)
    gout08 = B.sb("gout08", [128, 128], F32)
    NSS = 12
    ss = _RotBuf([B.sb("ss", [128, 8], F32) for _ in range(NSS)])
    ss2 = _RotBuf([B.sb("ss2", [128, 8], F32) for _ in range(NSS)])
    ss_res = [(B.res(), B.res()) for _ in range(NSS)]
    epsb = B.sb("epsb", [128, 1], F32)
    JS = ExitStack()
    junk = B.sb("junk", [128, 2048], BF16, JS)
    R = {n: B.res(n) for n in ["ident", "iota", "sv", "neglam", "gkq", "gout08", "ss", "ss2", "junk", "epsb"]}

    def rot_ss():
        ss.i = ss2.i = (ss.i + 1) % NSS
        R["ss"], R["ss2"] = ss_res[ss.i]

    rot_ss()
    B.op("vector", lambda e: e.memset(epsb[:], EPS), writes=[R["epsb"]])
    with ExitStack() as s0:
        rowi = B.sb("rowi", [128, 128], I32, s0)
        coli = B.sb("coli", [128, 128], I32, s0)
        rowf = B.sb("rowf", [128, 128], F32, s0)
        lt = B.sb("lt", [128, 128], F32, s0)
        r_rowi, r_coli, r_rowf, r_lt = B.res(), B.res(), B.res(), B.res()
        B.op("gpsimd", lambda e: e.iota(rowi[:], [[0, 128]], base=0, channel_multiplier=1), writes=[r_rowi])
        B.op("gpsimd", lambda e: e.iota(coli[:], [[1, 128]], base=0, channel_multiplier=0), writes=[r_coli])
        cp("vector", rowf[:], rowi[:], [r_rowi], [r_rowf])
        cp("vector", iota128[:], coli[:], [r_coli], [R["iota"]])
        B.op("vector", lambda e: e.tensor_tensor(out=ident_f[:], in0=rowf[:], in1=iota128[:], op=ALU.is_equal),
             [r_rowf, R["iota"]], [R["ident"]])
        cp("vector", ident_bf[:], ident_f[:], [R["ident"]], [R["ident"]])
        B.dma(sv[:], vecs[:, 6144:6144 + NSV], writes=[R["sv"]])
        B.op("vector", lambda e: e.tensor_tensor(out=lt[:, 0:64], in0=sv[:, LQ:LQ + 64], in1=sv[:, LQ + 64:LQ + 128], op=ALU.mult),
             [R["sv"]], [r_lt])
        B.op("vector", lambda e: e.tensor_tensor(out=lt[:, 64:128], in0=sv[:, LQ + 128:LQ + 192], in1=sv[:, LQ + 192:LQ + 256], op=ALU.mult),
             [R["sv"]], [r_lt])
        B.op("vector", lambda e: e.tensor_reduce(out=ss[:, 0:2], in_=lt[:].rearrange("p (a b) -> p a b", a=2), axis=AX.X, op=ALU.add),
             [r_lt], [R["ss"]])
        B.op("scalar", lambda e: e.activation(out=ss2[:, 0:2], in_=ss[:, 0:2], func=AF.Exp), [R["ss"]], [R["ss2"]])
        B.op("vector", lambda e: e.tensor_tensor(out=neglam[:], in0=ss2[:, 1:2], in1=ss2[:, 0:1], op=ALU.subtract),
             [R["ss2"]], [R["neglam"]])
        B.op("vector", lambda e: e.tensor_scalar(out=neglam[:], in0=neglam[:], scalar1=-0.2, scalar2=None, op0=ALU.add),
             [R["neglam"]], [R["neglam"]])
        for gi, off in enumerate([GK, GK, GQ, GQ]):
            B.op("vector", lambda e: e.tensor_scalar(out=gkq[:, gi, :], in0=sv[:, off:off + 64], scalar1=(1.0 if gi < 2 else 0.125), scalar2=None,
                                                     op0=ALU.mult), [R["sv"]], [R["gkq"]])
        B.op("vector", lambda e: e.tensor_scalar(out=gout08[:], in0=sv[:, GOUT:GOUT + 128], scalar1=0.8, scalar2=None, op0=ALU.mult),
             [R["sv"]], [R["gout08"]])
        B.barrier()

    def take_ss():
        rot_ss()
        i = ss.i
        return (ss.bufs[i], ss2.bufs[i], ss_res[i][0], ss_res[i][1])

    def rstd_from_ss(n, ncol, D, scale=None, ssb=None):
        if ssb is None:
            ssb = (ss.bufs[ss.i], ss2.bufs[ss.i], R["ss"], R["ss2"])
        s_, s2_, rs, rs2 = ssb
        B.op("scalar", lambda e: e.activation(out=s2_[0:n, 0:ncol], in_=s_[0:n, 0:ncol], func=AF.Ln, scale=1.0 / D, bias=epsb[0:n, 0:1]),
             [rs, R["epsb"]], [rs2])
        B.op("scalar", lambda e: e.activation(out=s2_[0:n, 0:ncol], in_=s2_[0:n, 0:ncol], func=AF.Exp, scale=-0.5), [rs2], [rs2])
        if scale is not None:
            c0, c1, sc = scale
            B.op("vector", lambda e: e.tensor_scalar(out=s2_[0:n, c0:c1], in0=s2_[0:n, c0:c1], scalar1=sc, scalar2=None, op0=ALU.mult),
                 [rs2], [rs2])

    def rms_rows(src, n, D, g_ap, g_res, dst, reads, writes):
        rot_ss()
        B.op("scalar", lambda e: e.activation(out=junk[0:n, 0:D], in_=src, func=AF.Square, accum_out=ss[0:n, 0:1]),
             reads, [B.res(), R["ss"]])
        rstd_from_ss(n, 1, D)
        B.op("vector", lambda e: e.scalar_tensor_tensor(out=dst, in0=src, scalar=ss2[0:n, 0:1], in1=g_ap, op0=ALU.mult, op1=ALU.mult),
             list(reads) + [R["ss2"], g_res], writes)

    def rms_rows_stages(src, n, D, g_ap, g_res, dst, reads, writes):
        ssb = take_ss()

        def A():
            B.op("scalar", lambda e: e.activation(out=junk[0:n, 0:D], in_=src, func=AF.Square, accum_out=ssb[0][0:n, 0:1]),
                 reads, [B.res(), ssb[2]])
            rstd_from_ss(n, 1, D, ssb=ssb)

        def Bs():
            B.op("vector", lambda e: e.scalar_tensor_tensor(out=dst, in0=src, scalar=ssb[1][0:n, 0:1], in1=g_ap, op0=ALU.mult, op1=ALU.mult),
                 list(reads) + [ssb[3], g_res], writes)
        return A, Bs

    def tr_bf(items, n_in, reads, banks):
        pt, pr = bank(banks)
        pb = pt[:].bitcast(BF16)
        fns = [(lambda e, j=j, a=a: e.transpose(pb[:, j * 128:j * 128 + n_in], a, ident_bf[0:n_in, 0:n_in])) for j, a in enumerate(items)]
        B.op("tensor", fns, list(reads) + [R["ident"]], [pr])
        v = pb[:, 0:len(items) * 128].rearrange("p (j t) -> p j t", t=128)[:, :, 0:n_in]
        return v, pr

    WST = ExitStack()
    NST = 2
    stage = [B.sb("stage%d" % i, [128, 2048], F32, WST) for i in range(NST)]
    stage_r = [B.res("stage%d" % i) for i in range(NST)]
    wl = [0]

    def wload(src, dst, dst_res, eng=None):
        i = wl[0] % NST
        if eng is None:
            eng = "gpsimd" if wl[0] % 2 == 0 else "scalar"
        wl[0] += 1
        nel = 1
        for d in src.shape[1:]:
            nel *= d
        st = stage[i][:, 0:nel]
        if len(src.shape) == 3:
            st = st.rearrange("p (a b) -> p a b", a=src.shape[1])
        B.dma(st, src, writes=[stage_r[i]])
        cp(eng, dst, st, [stage_r[i]], [dst_res])

    try:
        P03 = ExitStack()
        hT_own = B.sb("hT_own", [128, 16, OWN], BF16, P03)
        o_daT = B.sb("o_daT", [128, 8, OWN], BF16, P03)
        o_cmT = B.sb("o_cmT", [128, 4, OWN], BF16, P03)
        o_memT = B.sb("o_memT", [128, 4, OWN], BF16, P03)
        R_hTo, R_odaT, R_ocmT, R_omemT = B.res("hTo"), B.res("odaT"), B.res("ocmT"), B.res("omemT")
        PR1 = ExitStack()
        hT_B = B.sb("hT_B", [128, 16, NB], BF16, PR1, side="right")
        R_hTB = B.res("hTB")

        def hT(dc, c0, n):
            if c0 < OWN:
                return hT_own[:, dc, c0:c0 + n], R_hTo
            return hT_B[:, dc, c0 - OWN:c0 - OWN + n], R_hTB

        with ExitStack() as s0:
            gmix = B.sb("gmix", [128, 2048], F32, s0)
            R_gmix = B.res()
            B.dma(gmix[:], vecs[:, 0:2048], writes=[R_gmix])
            NX0 = 4
            xt = [B.sb("xt%d" % i, [128, 2048], F32, s0) for i in range(NX0)]
            hn = [B.sb("hn%d" % i, [128, 2048], BF16, s0) for i in range(NX0)]
            R_xt = [B.res() for _ in range(NX0)]
            R_hn = [B.res() for _ in range(NX0)]
            p0 = Pipe()
            for bi, (c0, n) in enumerate(TB_ALL):
                def mk0(bi=bi, c0=c0, n=n):
                    x_, hn_, rx, rh = xt[bi % NX0], hn[bi % NX0], R_xt[bi % NX0], R_hn[bi % NX0]
                    box = {}

                    def S0():
                        B.dma(x_[0:n, :], xa[c0:c0 + n, :], writes=[rx])
                        box["st"] = rms_rows_stages(x_[0:n, :], n, 2048, gmix[0:n, :], R_gmix, hn_[0:n, :], [rx], [rh])

                    def S1():
                        box["st"][0]()

                    def S2():
                        box["st"][1]()

                    def S3():
                        box["tr"] = []
                        for half in range(2):
                            items = [hn_[0:n, (half * 8 + j) * 128:(half * 8 + j + 1) * 128] for j in range(8)]
                            box["tr"].append(tr_bf(items, n, [rh], [0, 1, 2, 3, 4, 5, 6, 7]))

                    def S4():
                        for half in range(2):
                            v, pr = box["tr"][half]
                            if c0 < OWN:
                                dst, dr = hT_own[:, half * 8:half * 8 + 8, c0:c0 + n], R_hTo
                            else:
                                dst, dr = hT_B[:, half * 8:half * 8 + 8, c0 - OWN:c0 - OWN + n], R_hTB
                            cp("scalar" if half == 0 else "vector", dst, v, [pr], [dr])
                    return [S0, S1, S2, S3, S4]
                p0.push(mk0())
            p0.flush()
            B.barrier()
        if stop_after <= 0:
            raise _Stop()

        PROJ, TRB, SCB, OB0, OB1 = [0, 1], [2, 7], [3, 4], [5], [6]
        with ExitStack() as s1:
            HS = 2
            wk = [B.sb("wk", [128, 16, 128], BF16, s1) for _ in range(HS)]
            wv = [B.sb("wv", [128, 16, 128], BF16, s1) for _ in range(HS)]
            wq = [B.sb("wq", [128, 16, 128], BF16, s1) for _ in range(HS)]
            R_wk, R_wv, R_wq = [B.res() for _ in range(HS)], [B.res() for _ in range(HS)], [B.res() for _ in range(HS)]
            kT = [B.sb("kT", [128, OWN + NB], BF16, s1) for _ in range(HS)]
            kTs = [B.sb("kTs", [128, 1024], BF16, s1) for _ in range(HS)]
            Vb = [B.sb("Vb", [128, 17, 130], BF16, s1) for _ in range(HS)]
            Vs = [B.sb("Vs", [128, 8, 130], BF16, s1) for _ in range(HS)]
            qT = [B.sb("qT", [128, 2, OWN], BF16, s1) for _ in range(HS)]
            R_kT, R_kTs, R_Vb, R_Vs, R_qT = ([B.res() for _ in range(HS)] for _ in range(5))
            cs = B.sb("cs", [128, 17, 16], F32, s1)
            cm = B.sb("cm", [128, 8, 128], F32, s1)
            R_cs, R_cm = B.res("cs"), B.res("cm")
            B.dma(cs[:], cs_d[:, :, :], writes=[R_cs])
            B.dma(cm[:], cmask_d[:, :, :], writes=[R_cm])
            for s_ in range(HS):
                B.op("vector", lambda e: e.memset(qT[s_][:], 0.0), writes=[R_qT[s_]])
                B.op("gpsimd", lambda e: e.memset(Vb[s_][:, :, 128:130], 1.0), writes=[R_Vb[s_]])
                B.op("gpsimd", lambda e: e.memset(Vs[s_][:, :, 128:130], 1.0), writes=[R_Vs[s_]])
            ckb = B.sb("ckb", [128, 8, 128], BF16, s1)
            R_ckb = B.res("ckb")
            NKQ = 8
            sq = [B.sb("sq", [128, 256], F32, s1) for _ in range(3)]
            R_sq = [B.res() for _ in range(3)]
            kq = [B.sb("kq%d" % i, [128, 4, 64], F32, s1) for i in range(NKQ)]
            R_kq = [B.res("kq%d" % i) for i in range(NKQ)]
            rt = [B.sb("rt", [128, 4, 4, 8], F32, s1) for _ in range(2)]
            R_rt = [B.res(), B.res()]
            kqb = [B.sb("kqb", [128, 256], BF16, s1) for _ in range(3)]
            R_kqb = [B.res() for _ in range(3)]
            vf = [B.sb("vf%d" % i, [128, 128], F32, s1) for i in range(3)]
            R_vf = [B.res("vf%d" % i) for i in range(3)]
            NPT = 4
            PT = [B.sb("PT%d" % i, [128, 256], BF16, s1) for i in range(NPT)]
            R_PT = [B.res("PT%d" % i) for i in range(NPT)]
            NFZ = 4
            rz = [B.sb("rz", [128, 4], F32, s1) for _ in range(NFZ)]
            R_rz = [B.res() for _ in range(NFZ)]
            of = [B.sb("of", [128, 128], F32, s1) for _ in range(NFZ)]
            R_of = [B.res() for _ in range(NFZ)]
            ob = [B.sb("ob", [128, 128], BF16, s1) for _ in range(NFZ)]
            R_ob = [B.res() for _ in range(NFZ)]
            pp = Pipe()
            ptc = [0]
            kqc = [0]
            fzc = [0]

            def da_finalize(ots, orrs, n, h, qc0):
                ot, ot1 = ots
                orr, orr1 = orrs
                f = fzc[0] % NFZ
                fzc[0] += 1
                rz_, rrz, of_, rof, ob_, rob = rz[f], R_rz[f], of[f], R_of[f], ob[f], R_ob[f]
                ssb = take_ss()
                box = {}

                def F1():
                    B.op("vector", lambda e: e.reciprocal(out=rz_[0:n, 0:1], in_=ot[0:n, 128:129]), [orr], [rrz])
                    B.op("vector", lambda e: e.reciprocal(out=rz_[0:n, 1:2], in_=ot1[0:n, 128:129]), [orr1], [rrz])
                    B.op("vector", lambda e: e.tensor_tensor(out=rz_[0:n, 2:3], in0=rz_[0:n, 1:2], in1=neglam[0:n, 0:1], op=ALU.mult),
                         [rrz, R["neglam"]], [rrz])
                    B.op("vector", lambda e: e.tensor_scalar(out=of_[0:n, :], in0=ot[0:n, 0:128], scalar1=rz_[0:n, 0:1], scalar2=None, op0=ALU.mult),
                         [orr, rrz], [rof])
                    B.op("vector", lambda e: e.scalar_tensor_tensor(out=of_[0:n, :], in0=ot1[0:n, 0:128], scalar=rz_[0:n, 2:3], in1=of_[0:n, :],
                                                                    op0=ALU.mult, op1=ALU.add), [orr1, rrz, rof], [rof])

                def F2():
                    B.op("scalar", lambda e: e.activation(out=junk[0:n, 0:128], in_=of_[0:n, :], func=AF.Square, accum_out=ssb[0][0:n, 0:1]),
                         [rof], [B.res(), ssb[2]])
                    rstd_from_ss(n, 1, 128, ssb=ssb)

                def F3():
                    B.op("vector", lambda e: e.scalar_tensor_tensor(out=ob_[0:n, :], in0=of_[0:n, :], scalar=ssb[1][0:n, 0:1], in1=gout08[0:n, :],
                                                                    op0=ALU.mult, op1=ALU.mult), [rof, ssb[3], R["gout08"]], [rob])

                def F4():
                    box["tr"] = tr_bf([ob_[0:n, :]], n, [rob], TRB)

                def F5():
                    v, pr = box["tr"]
                    cp("scalar", o_daT[:, h, qc0:qc0 + n], v[:, 0, :], [pr], [R_odaT])

                pp.push([F1, F2, F3, F4, F5])

            def head_setup(h):
                s_ = h % HS
                wload(w_in_b[8 + h], wk[s_][:], R_wk[s_])
                wload(w_in_b[16 + h], wv[s_][:], R_wv[s_])
                wload(w_in_b[h], wq[s_][:], R_wq[s_])
                wload(c_dak.rearrange("(b p) c -> p b c", p=128)[:, :, h * 128:(h + 1) * 128], ckb[:], R_ckb)
                wload(c_dav.rearrange("(b p) c -> p b c", p=128)[:, :, h * 128:(h + 1) * 128], Vs[s_][:, :, 0:128], R_Vs[s_])
                v, pr = tr_bf([ckb[:, j, :] for j in range(8)], 128, [R_ckb], TRB)
                cp("scalar", kTs[s_][:].rearrange("p (j t) -> p j t", t=128), v, [pr], [R_kTs[s_]])

            def proj_stages(h, bi):
                s_ = h % HS
                c0, n = TB_ALL[bi]
                own = c0 < OWN
                ng = 4 if own else 2
                w_ = ng * 64
                kc = kqc[0]
                kqc[0] += 1
                kq_, rkq = kq[kc % NKQ], R_kq[kc % NKQ]
                vf_, rvf = vf[kc % 3], R_vf[kc % 3]
                sq_, rsq = sq[kc % 3], R_sq[kc % 3]
                rt_, rrt = rt[kc % 2], R_rt[kc % 2]
                kqb_, rkqb = kqb[kc % 3], R_kqb[kc % 3]
                st8 = {}

                def S1():
                    pt, pr = bank(PROJ)
                    st8["pt"] = (pt, pr)
                    fns = []
                    groups = [(0, wk[s_]), (256, wv[s_])] + ([(128, wq[s_])] if own else [])
                    for (pc0, wt) in groups:
                        for dc in range(16):
                            a, rh = hT(dc, c0, n)
                            fns.append(lambda e, a=a, wt=wt, dc=dc, pc0=pc0: e.matmul(pt[0:n, pc0:pc0 + 128], lhsT=a, rhs=wt[:, dc, :],
                                                                                    start=(dc == 0), stop=(dc == 15)))
                    B.op("tensor", fns, [rh, R_wk[s_], R_wv[s_], R_wq[s_]], [pr])

                def S2():
                    pt, pr = st8["pt"]
                    B.op("scalar", lambda e: e.activation(out=sq_[0:n, 0:w_], in_=pt[0:n, 0:w_], func=AF.Square), [pr], [rsq])
                    cp("scalar", kq_[0:n, 0:ng, :], pt[0:n, 0:w_].rearrange("p (g d) -> p g d", d=64), [pr], [rkq])
                    cp("scalar", Vb[s_][0:n, bi, 0:128], pt[0:n, 256:384], [pr], [R_Vb[s_]])
                    if own:
                        cp("scalar", vf_[0:n, :], pt[0:n, 256:384], [pr], [rvf])
                        B.dma(ov_o[c0:c0 + n, h * 128:(h + 1) * 128], vf_[0:n, :], reads=[rvf], q="scalar")

                def S3():
                    ssb = take_ss()
                    st8["ssb"] = ssb
                    B.op("vector", lambda e: e.tensor_reduce(out=ssb[0][0:n, 0:ng], in_=sq_[0:n, 0:w_].rearrange("p (g d) -> p g d", d=64),
                                                             axis=AX.X, op=ALU.add), [rsq], [ssb[2]])

                def S4():
                    rstd_from_ss(n, ng, 64, ssb=st8["ssb"])

                def S5():
                    ssb = st8["ssb"]
                    B.op("vector", lambda e: e.tensor_tensor(out=kq_[0:n, 0:ng, :], in0=kq_[0:n, 0:ng, :],
                                                             in1=ssb[1][0:n, 0:ng].unsqueeze(2).to_broadcast([n, ng, 64]), op=ALU.mult),
                         [rkq, ssb[3]], [rkq])
                    B.op("vector", lambda e: e.tensor_tensor(out=kq_[0:n, 0:ng, :], in0=kq_[0:n, 0:ng, :], in1=gkq[0:n, 0:ng, :], op=ALU.mult),
                         [rkq, R["gkq"]], [rkq])
                    x1 = kq_[0:n, 0:ng, 0:8]
                    x2 = kq_[0:n, 0:ng, 8:16]
                    cosb = cs[0:n, bi, 0:8].unsqueeze(1).to_broadcast([n, ng, 8])
                    sinb = cs[0:n, bi, 8:16].unsqueeze(1).to_broadcast([n, ng, 8])
                    for ti, (xx, tb_) in enumerate([(x1, cosb), (x2, sinb), (x2, cosb), (x1, sinb)]):
                        B.op("vector", lambda e: e.tensor_tensor(out=rt_[0:n, ti, 0:ng, :], in0=xx, in1=tb_, op=ALU.mult), [rkq, R_cs], [rrt])
                    B.op("vector", lambda e: e.tensor_tensor(out=x1, in0=rt_[0:n, 0, 0:ng, :], in1=rt_[0:n, 1, 0:ng, :], op=ALU.subtract), [rrt], [rkq])
                    B.op("vector", lambda e: e.tensor_tensor(out=x2, in0=rt_[0:n, 2, 0:ng, :], in1=rt_[0:n, 3, 0:ng, :], op=ALU.add), [rrt], [rkq])

                def S6():
                    cp("scalar", kqb_[0:n, 0:w_], kq_[0:n, 0:ng, :].rearrange("p g d -> p (g d)"), [rkq], [rkqb])
                    if own:
                        B.dma(ok_o[c0:c0 + n, h * 128:(h + 1) * 128], kq_[0:n, 0:2, :].rearrange("p g d -> p (g d)"), reads=[rkq], q="scalar")

                def S7():
                    items = [kqb_[0:n, 0:128]] + ([kqb_[0:n, 128:256]] if own else [])
                    st8["tr"] = tr_bf(items, n, [rkqb], TRB)

                def S8():
                    v, prt = st8["tr"]
                    cp("vector", kT[s_][:, c0:c0 + n], v[:, 0, :], [prt], [R_kT[s_]])
                    if own:
                        cp("vector", qT[s_][0:64, 0, c0:c0 + n], v[0:64, 1, :], [prt], [R_qT[s_]])
                        cp("vector", qT[s_][64:128, 1, c0:c0 + n], v[64:128, 1, :], [prt], [R_qT[s_]])

                return [S1, S2, S3, S4, S5, S6, S7, S8]

            def attn_unit(h, i):
                s_ = h % HS
                ot0, orr0 = bank(OB0)
                ot1, orr1 = bank(OB1)
                ots, orrs = (ot0, ot1), (orr0, orr1)
                ap_ = Pipe()
                if i < 8:
                    keys = [("A", j) for j in range(i + 1)] + [("B", j) for j in range(i + 1)]
                    nkeys = len(keys)
                    nq, q0 = 128, i * 128
                else:
                    keys = [("P", j) for j in range(8)] + [("N", 0)]
                    nkeys = 9
                    nq, q0 = 64, 1024
                for ki, (kind, j) in enumerate(keys):
                    def mk(ki=ki, kind=kind, j=j):
                        nk = 64 if kind == "N" else 128
                        if kind == "A":
                            lh, rv, rr = kT[s_][:, j * 128:(j + 1) * 128], Vb[s_][0:nk, j, 0:129], [R_kT[s_], R_Vb[s_]]
                        elif kind == "B":
                            lh, rv, rr = kT[s_][:, OWN + j * 128:OWN + (j + 1) * 128], Vb[s_][0:nk, 9 + j, 0:129], [R_kT[s_], R_Vb[s_]]
                        elif kind == "P":
                            lh, rv, rr = kTs[s_][:, j * 128:(j + 1) * 128], Vs[s_][0:nk, j, 0:129], [R_kTs[s_], R_Vs[s_]]
                        else:
                            lh, rv, rr = kT[s_][:, 1024:1088], Vb[s_][0:nk, 8, 0:129], [R_kT[s_], R_Vb[s_]]
                        box = {}

                        def A1():
                            st, sr = bank(SCB)
                            fns = [(lambda e, m=m: e.matmul(st[0:nk, m * nq:(m + 1) * nq], lhsT=lh, rhs=qT[s_][:, m, q0:q0 + nq],
                                                            start=True, stop=True)) for m in range(2)]
                            B.op("tensor", fns, [rr[0], R_qT[s_]], [sr])
                            P_, rp = PT[ptc[0] % NPT], R_PT[ptc[0] % NPT]
                            ptc[0] += 1
                            box["P"] = (P_, rp)
                            B.op("scalar", lambda e: e.activation(out=P_[0:nk, 0:2 * nq], in_=st[0:nk, 0:2 * nq], func=AF.Exp), [sr], [rp])
                            if i < 8 and j == i:
                                if kind == "A":
                                    B.op("gpsimd", lambda e: e.memset(P_[64:128, :].rearrange("p (m q) -> p m q", m=2)[:, :, 0:64], 0.0), [], [rp])
                                else:
                                    B.op("vector", lambda e: e.tensor_tensor(out=P_[:, :].rearrange("p (m q) -> p m q", m=2),
                                                                             in0=P_[:, :].rearrange("p (m q) -> p m q", m=2),
                                                                             in1=cm[:, i, :].unsqueeze(1).to_broadcast([128, 2, 128]), op=ALU.mult),
                                         [R_cm], [rp])

                        def A2():
                            P_, rp = box["P"]
                            fns = [(lambda e, m=m: e.matmul(ots[m][0:nq, 0:129], lhsT=P_[0:nk, m * nq:(m + 1) * nq], rhs=rv,
                                                            start=(ki == 0), stop=(ki == nkeys - 1))) for m in range(2)]
                            B.op("tensor", fns, [rp, rr[1]], [orr0, orr1])
                        return [A1, A2]
                    ap_.push(mk())
                ap_.flush()
                da_finalize(ots, orrs, nq, h, q0)

            head_setup(0)
            for bi in range(17):
                pp.push(proj_stages(0, bi))
            pp.flush()
            for h in range(8):
                if stop_after == 0.5 and h == 1:
                    raise _Stop()
                nxt = h + 1 < 8
                if nxt:
                    head_setup(h + 1)
                pb_i = 0
                for u in range(9):
                    attn_unit(h, u)
                    if nxt:
                        for _ in range(2 if u < 8 else 1):
                            pp.push(proj_stages(h + 1, pb_i))
                            pb_i += 1
                assert (not nxt) or pb_i == 17
                pp.flush()
            B.barrier()
        PR1.close()
        if stop_after <= 1:
            raise _Stop()

        with ExitStack() as s2:
            PROJ, TRB, SCB = [0, 1, 2], [3], [4, 5]
            mkT = [B.sb("mkT%d" % i, [128, 4, 256], BF16, s2) for i in range(2)]
            mvb = [B.sb("mvb%d" % i, [128, 2, 4, 130], BF16, s2) for i in range(2)]
            R_mkT = [B.res(), B.res()]
            R_mvb = [B.res(), B.res()]
            for i in range(2):
                B.op("gpsimd", lambda e, i=i: e.memset(mvb[i][:, :, :, 128:130], 1.0), writes=[R_mvb[i]])
            t512 = [B.sb("t512_%d" % i, [128, 512], F32, s2) for i in range(4)]
            R_t512 = [B.res() for i in range(4)]
            b512 = [B.sb("b512_%d" % i, [128, 512], BF16, s2) for i in range(3)]
            R_b512 = [B.res() for i in range(3)]
            with ExitStack() as s2a:
                gmem = B.sb("gmem", [128, 2048], F32, s2a)
                R_gmem = B.res()
                B.dma(gmem[:], vecs[:, 2048:4096], writes=[R_gmem])
                mxs = B.sb("mxs", [128, 2048], F32, s2a)
                mnb = B.sb("mnb", [128, 2048], BF16, s2a)
                mT = B.sb("mT", [128, 16, 256], BF16, s2a)
                R_mxs, R_mnb, R_mT = B.res(), B.res(), B.res()
                wmk = B.sb("wmk", [128, 16, 512], BF16, s2a)
                wmv = B.sb("wmv", [128, 16, 512], BF16, s2a)
                R_wmk, R_wmv = B.res(), B.res()
                for j in range(4):
                    wload(w_mkv_b[j], wmk[:, :, j * 128:(j + 1) * 128], R_wmk)
                for j in range(4):
                    wload(w_mkv_b[4 + j], wmv[:, :, j * 128:(j + 1) * 128], R_wmv)
                for mb in range(2):
                    B.dma(mxs[:], mem_x[mb * 128:(mb + 1) * 128, :], writes=[R_mxs])
                    rms_rows(mxs[:], 128, 2048, gmem[:], R_gmem, mnb[:], [R_mxs], [R_mnb])
                    for half in range(2):
                        v, pr = tr_bf([mnb[:, (half * 8 + j) * 128:(half * 8 + j + 1) * 128] for j in range(8)], 128, [R_mnb], TRB)
                        cp("scalar" if half == 0 else "vector", mT[:, half * 8:half * 8 + 8, mb * 128:(mb + 1) * 128], v, [pr], [R_mT])
                for mb in range(2):
                    pk, prk = bank(PROJ)
                    B.op("tensor", [(lambda e, dc=dc: e.matmul(pk[:, :], lhsT=mT[:, dc, mb * 128:(mb + 1) * 128], rhs=wmk[:, dc, :],
                                                               start=(dc == 0), stop=(dc == 15))) for dc in range(16)], [R_mT, R_wmk], [prk])
                    pv, prv = bank(PROJ)
                    B.op("tensor", [(lambda e, dc=dc: e.matmul(pv[:, :], lhsT=mT[:, dc, mb * 128:(mb + 1) * 128], rhs=wmv[:, dc, :],
                                                               start=(dc == 0), stop=(dc == 15))) for dc in range(16)], [R_mT, R_wmv], [prv])
                    tq, rtq = t512[0], R_t512[0]
                    tk, rtk = t512[1], R_t512[1]
                    B.op("scalar", lambda e: e.activation(out=tq[:, :], in_=pk[:, :], func=AF.Square), [prk], [rtq])
                    B.op("vector", lambda e: e.tensor_reduce(out=ss[:, 0:4], in_=tq[:, :].rearrange("p (g d) -> p g d", d=128), axis=AX.X, op=ALU.add),
                         [rtq], [R["ss"]])
                    rstd_from_ss(128, 4, 128)
                    B.op("vector", lambda e: e.tensor_tensor(out=tk[:, :].rearrange("p (g d) -> p g d", d=128),
                                                             in0=pk[:, :].rearrange("p (g d) -> p g d", d=128),
                                                             in1=ss2[:, 0:4].unsqueeze(2).to_broadcast([128, 4, 128]), op=ALU.mult), [prk, R["ss2"]], [rtk])
                    B.op(POOL_TT, lambda e: e.tensor_tensor(out=tk[:, :].rearrange("p (g d) -> p g d", d=128),
                                                             in0=tk[:, :].rearrange("p (g d) -> p g d", d=128),
                                                             in1=sv[:, MKG:MKG + 128].unsqueeze(1).to_broadcast([128, 4, 128]), op=ALU.mult),
                         [rtk, R["sv"]], [rtk])
                    B.dma(omk_o[mb * 128:(mb + 1) * 128, :], tk[:, :], reads=[rtk], q="scalar")
                    cp("scalar", b512[0][:, :], tk[:, :], [rtk], [R_b512[0]])
                    v, pr = tr_bf([b512[0][:, g * 128:(g + 1) * 128] for g in range(4)], 128, [R_b512[0]], TRB)
                    cp("vector", mkT[0][:, :, mb * 128:(mb + 1) * 128], v, [pr], [R_mkT[0]])
                    tv, rtv = t512[2], R_t512[2]
                    cp("vector", tv[:, :], pv[:, :], [prv], [rtv])
                    B.dma(omv_o[mb * 128:(mb + 1) * 128, :], tv[:, :], reads=[rtv], q="scalar")
                    cp("scalar", mvb[0][:, mb, :, 0:128], pv[:, :].rearrange("p (g d) -> p g d", d=128), [prv], [R_mvb[0]])
                B.barrier()
            for mb in range(2):
                tk, rtk = t512[1], R_t512[1]
                B.dma(tk[:, :], c_mk[mb * 128:(mb + 1) * 128, :], writes=[rtk])
                cp("gpsimd", b512[0][:, :], tk[:, :], [rtk], [R_b512[0]])
                v, pr = tr_bf([b512[0][:, g * 128:(g + 1) * 128] for g in range(4)], 128, [R_b512[0]], TRB)
                cp("vector", mkT[1][:, :, mb * 128:(mb + 1) * 128], v, [pr], [R_mkT[1]])
                tv, rtv = t512[2], R_t512[2]
                B.dma(tv[:, :], c_mv[mb * 128:(mb + 1) * 128, :], writes=[rtv])
                cp("gpsimd", mvb[1][:, mb, :, 0:128], tv[:, :].rearrange("p (g d) -> p g d", d=128), [rtv], [R_mvb[1]])
            wsf = B.sb("wsf", [128, 4, 128], F32, s2)
            wsb = B.sb("wsb", [128, 4, 128], BF16, s2)
            wsT = B.sb("wsT", [128, 4, 128], BF16, s2)
            bs_sb = B.sb("bs_sb", [128, 4], F32, s2)
            tril = B.sb("tril", [128, 128], F32, s2)
            rowf2 = B.sb("rowf2", [128, 1], F32, s2)
            rowi2 = B.sb("rowi2", [128, 1], I32, s2)
            R_wsf, R_wsb, R_wsT, R_bs, R_tril, R_row = B.res(), B.res(), B.res(), B.res(), B.res(), B.res()
            B.dma(wsf[:], wsl[:, :, :], writes=[R_wsf])
            B.dma(bs_sb[:], bsT[:, :], writes=[R_bs])
            B.op("gpsimd", lambda e: e.iota(rowi2[:], [[0, 1]], base=0, channel_multiplier=1), writes=[R_row])
            cp("vector", rowf2[:], rowi2[:], [R_row], [R_row])
            B.op("vector", lambda e: e.tensor_scalar(out=tril[:], in0=iota128[:], scalar1=rowf2[:, 0:1], scalar2=None, op0=ALU.is_le),
                 [R["iota"], R_row], [R_tril])
            B.op("vector", lambda e: e.tensor_tensor(out=wsb[:], in0=wsf[:], in1=tril[:].unsqueeze(1).to_broadcast([128, 4, 128]), op=ALU.mult),
                 [R_wsf, R_tril], [R_wsb])
            v, pr = tr_bf([wsb[:, g, :] for g in range(4)], 128, [R_wsb], TRB)
            cp("vector", wsT[:], v, [pr], [R_wsT])
            wu = B.sb("wu", [128, 16, 512], BF16, s2)
            wvc = B.sb("wvc", [128, 16, 512], BF16, s2)
            wqm = B.sb("wqm", [128, 16, 512], BF16, s2)
            R_wu, R_wvc, R_wqm = B.res(), B.res(), B.res()
            for j in range(4):
                wload(w_in_b[24 + j], wu[:, :, j * 128:(j + 1) * 128], R_wu)
                wload(w_in_b[28 + j], wvc[:, :, j * 128:(j + 1) * 128], R_wvc)
                wload(w_in_b[32 + j], wqm[:, :, j * 128:(j + 1) * 128], R_wqm)
            qmT = B.sb("qmT", [128, 4, 128], BF16, s2)
            R_qmT = B.res()
            PM = [B.sb("PM%d" % i, [128, 2, 128], BF16, s2) for i in range(2)]
            R_PM = [B.res(), B.res()]
            pmc = [0]
            for bi, (c0, n) in enumerate(TB_OWN):
                sm = 0 if bi < 8 else 1
                pu, pru = bank(PROJ)
                B.op("tensor", [(lambda e, dc=dc: e.matmul(pu[0:n, :], lhsT=hT_own[:, dc, c0:c0 + n], rhs=wu[:, dc, :],
                                                           start=(dc == 0), stop=(dc == 15))) for dc in range(16)], [R_hTo, R_wu], [pru])
                uf, ruf = t512[0], R_t512[0]
                B.op("scalar", lambda e: e.activation(out=uf[0:n, :], in_=pu[0:n, :], func=AF.Gelu), [pru], [ruf])
                pv, prv = bank(PROJ)
                B.op("tensor", [(lambda e, dc=dc: e.matmul(pv[0:n, :], lhsT=hT_own[:, dc, c0:c0 + n], rhs=wvc[:, dc, :],
                                                           start=(dc == 0), stop=(dc == 15))) for dc in range(16)], [R_hTo, R_wvc], [prv])
                vg, rvg = t512[1], R_t512[1]
                B.op("scalar", lambda e: e.activation(out=vg[0:n, :], in_=pv[0:n, :], func=AF.Gelu, accum_out=ss[0:n, 0:1]), [prv], [rvg, R["ss"]])
                B.op("vector", lambda e: e.tensor_scalar(out=ss2[0:n, 1:2], in0=ss[0:n, 0:1], scalar1=1.0 / 512, scalar2=None, op0=ALU.mult),
                     [R["ss"]], [R["ss2"]])
                B.op("vector", lambda e: e.tensor_scalar(out=vg[0:n, :], in0=vg[0:n, :], scalar1=ss2[0:n, 1:2], scalar2=None, op0=ALU.subtract),
                     [rvg, R["ss2"]], [rvg])
                B.op("scalar", lambda e: e.activation(out=junk[0:n, 0:512], in_=vg[0:n, :], func=AF.Square, accum_out=ss[0:n, 0:1]),
                     [rvg], [B.res(), R["ss"]])
                rstd_from_ss(n, 1, 512)
                vn, rvn = t512[2], R_t512[2]
                B.op("vector", lambda e: e.scalar_tensor_tensor(out=vn[0:n, :], in0=vg[0:n, :], scalar=ss2[0:n, 0:1], in1=sv[0:n, LNG:LNG + 512],
                                                                op0=ALU.mult, op1=ALU.mult), [rvg, R["ss2"], R["sv"]], [rvn])
                B.op(POOL_TT, lambda e: e.tensor_tensor(out=vn[0:n, :], in0=vn[0:n, :], in1=sv[0:n, LNB:LNB + 512], op=ALU.add), [rvn, R["sv"]], [rvn])
                if sm == 1:
                    B.dma(ocv_o[:, :], vn[0:n, :], reads=[rvn], q="scalar")
                cp("scalar", b512[0][0:n, :], vn[0:n, :], [rvn], [R_b512[0]])
                pc, prc = bank(SCB)
                B.op("tensor", [(lambda e, g=g: e.matmul(pc[0:n, g * 128:(g + 1) * 128], lhsT=wsT[0:n, g, 0:n], rhs=b512[0][0:n, g * 128:(g + 1) * 128],
                                                         start=True, stop=True)) for g in range(4)], [R_wsT, R_b512[0]], [prc])
                for g in range(4):
                    B.op("vector", lambda e, g=g: e.scalar_tensor_tensor(out=b512[1][0:n, g * 128:(g + 1) * 128], in0=pc[0:n, g * 128:(g + 1) * 128],
                                                                         scalar=bs_sb[0:n, g:g + 1], in1=uf[0:n, g * 128:(g + 1) * 128],
                                                                         op0=ALU.add, op1=ALU.mult), [prc, R_bs, ruf], [R_b512[1]])
                v, pr = tr_bf([b512[1][0:n, g * 128:(g + 1) * 128] for g in range(4)], n, [R_b512[1]], TRB)
                cp("scalar", o_cmT[:, :, c0:c0 + n], v, [pr], [R_ocmT])
                pq, prq = bank(PROJ)
                B.op("tensor", [(lambda e, dc=dc: e.matmul(pq[0:n, :], lhsT=hT_own[:, dc, c0:c0 + n], rhs=wqm[:, dc, :],
                                                           start=(dc == 0), stop=(dc == 15))) for dc in range(16)], [R_hTo, R_wqm], [prq])
                tq, rtq = t512[3], R_t512[3]
                B.op("scalar", lambda e: e.activation(out=tq[0:n, :], in_=pq[0:n, :], func=AF.Square), [prq], [rtq])
                B.op("vector", lambda e: e.tensor_reduce(out=ss[0:n, 0:4], in_=tq[0:n, :].rearrange("p (g d) -> p g d", d=128), axis=AX.X, op=ALU.add),
                     [rtq], [R["ss"]])
                rstd_from_ss(n, 4, 128, scale=(0, 4, 128 ** -0.5))
                B.op("vector", lambda e: e.tensor_tensor(out=tq[0:n, :].rearrange("p (g d) -> p g d", d=128),
                                                         in0=pq[0:n, :].rearrange("p (g d) -> p g d", d=128),
                                                         in1=ss2[0:n, 0:4].unsqueeze(2).to_broadcast([n, 4, 128]), op=ALU.mult), [prq, R["ss2"], rtq], [rtq])
                B.op(POOL_TT, lambda e: e.tensor_tensor(out=b512[2][0:n, :].rearrange("p (g d) -> p g d", d=128),
                                                         in0=tq[0:n, :].rearrange("p (g d) -> p g d", d=128),
                                                         in1=sv[0:n, MQG:MQG + 128].unsqueeze(1).to_broadcast([n, 4, 128]), op=ALU.mult),
                     [rtq, R["sv"]], [R_b512[2]])
                v, pr = tr_bf([b512[2][0:n, g * 128:(g + 1) * 128] for g in range(4)], n, [R_b512[2]], TRB)
                cp("vector", qmT[:, :, 0:n], v, [pr], [R_qmT])
                for g in range(4):
                    st, sr = bank(SCB)
                    B.op("tensor", [(lambda e, mb=mb: e.matmul(st[:, mb * 128:mb * 128 + n], lhsT=mkT[sm][:, g, mb * 128:(mb + 1) * 128],
                                                               rhs=qmT[:, g, 0:n], start=True, stop=True)) for mb in range(2)],
                         [R_mkT[sm], R_qmT], [sr])
                    P_, rp = PM[pmc[0] % 2], R_PM[pmc[0] % 2]
                    pmc[0] += 1
                    B.op("scalar", lambda e: e.activation(out=P_[:, :, 0:n], in_=st[:, 0:256].rearrange("p (m q) -> p m q", m=2)[:, :, 0:n], func=AF.Exp),
                         [sr], [rp])
                    ot, orr = bank([6, 7])
                    B.op("tensor", [(lambda e, mb=mb: e.matmul(ot[0:n, 0:129], lhsT=P_[:, mb, 0:n], rhs=mvb[sm][:, mb, g, 0:129],
                                                               start=(mb == 0), stop=(mb == 1))) for mb in range(2)], [rp, R_mvb[sm]], [orr])
                    B.op("vector", lambda e: e.reciprocal(out=ss2[0:n, 4:5], in_=ot[0:n, 128:129]), [orr], [R["ss2"]])
                    B.op("vector", lambda e, g=g: e.tensor_scalar(out=b512[1][0:n, g * 128:(g + 1) * 128], in0=ot[0:n, 0:128], scalar1=ss2[0:n, 4:5],
                                                                  scalar2=None, op0=ALU.mult), [orr, R["ss2"]], [R_b512[1]])
                v, pr = tr_bf([b512[1][0:n, g * 128:(g + 1) * 128] for g in range(4)], n, [R_b512[1]], TRB)
                cp("scalar", o_memT[:, :, c0:c0 + n], v, [pr], [R_omemT])
            B.barrier()

        if debug:
            B.dma(dbg_br[:, 0:8, :], o_daT[:], reads=[R_odaT], q="scalar")
            B.dma(dbg_br[:, 8:12, :], o_cmT[:], reads=[R_ocmT], q="scalar")
            B.dma(dbg_br[:, 12:16, :], o_memT[:], reads=[R_omemT], q="scalar")
        if stop_after <= 2:
            raise _Stop()
        PR3 = ExitStack()
        mergedT = B.sb("mergedT", [128, 16, OWN], BF16, PR3, side="right")
        R_mT3 = B.res("mergedT")
        with ExitStack() as s3:
            bg = B.sb("bg", [128, 48], F32, s3)
            R_bg = B.res()
            B.dma(bg[:], bgT[:, :], writes=[R_bg])
            NW = 2
            wg = [[B.sb("wg%d_%d" % (i, b), [128, 16, 128], BF16, s3) for b in range(3)] for i in range(NW)]
            wb = [B.sb("wb%d" % i, [128, 16, 128], BF16, s3) for i in range(NW)]
            R_wg = [[B.res() for b in range(3)] for i in range(NW)]
            R_wb = [B.res() for i in range(NW)]
            G = [B.sb("G%d" % b, [128, OWN], BF16, s3) for b in range(3)]
            R_G = [B.res() for b in range(3)]
            macc = B.sb("macc", [128, OWN], F32, s3)
            mtmp = B.sb("mtmp", [128, OWN], F32, s3)
            R_macc, R_mtmp = B.res(), B.res()
            GB, BB = [0, 1, 2, 3], [4, 5, 6, 7]
            for fc in range(16):
                wi = fc % NW
                for b in range(3):
                    wload(w_in_b[36 + b * 16 + fc], wg[wi][b][:], R_wg[wi][b])
                wload(w_br_b[fc], wb[wi][:], R_wb[wi])
                for b in range(3):
                    for (t0, tn) in TSL:
                        pg, prg = bank(GB)
                        B.op("tensor", [(lambda e, dc=dc: e.matmul(pg[:, 0:tn], lhsT=wg[wi][b][:, dc, :], rhs=hT_own[:, dc, t0:t0 + tn],
                                                                   start=(dc == 0), stop=(dc == 15))) for dc in range(16)], [R_wg[wi][b], R_hTo], [prg])
                        B.op("scalar", lambda e: e.activation(out=G[b][:, t0:t0 + tn], in_=pg[:, 0:tn], func=AF.Sigmoid,
                                                              bias=bg[:, b * 16 + fc:b * 16 + fc + 1]), [prg, R_bg], [R_G[b]])
                srcs = [(o_daT, R_odaT, 0, 8), (o_cmT, R_ocmT, 8, 4), (o_memT, R_omemT, 12, 4)]
                for b, (src, rsrc, k0, nk) in enumerate(srcs):
                    for (t0, tn) in TSL:
                        pb_, prb = bank(BB)
                        B.op("tensor", [(lambda e, kc=kc: e.matmul(pb_[:, 0:tn], lhsT=wb[wi][:, k0 + kc, :], rhs=src[:, kc, t0:t0 + tn],
                                                                   start=(kc == 0), stop=(kc == nk - 1))) for kc in range(nk)], [R_wb[wi], rsrc], [prb])
                        if b == 0:
                            B.op("vector", lambda e: e.tensor_tensor(out=macc[:, t0:t0 + tn], in0=pb_[:, 0:tn], in1=G[0][:, t0:t0 + tn], op=ALU.mult),
                                 [prb, R_G[0]], [R_macc])
                        else:
                            B.op("vector", lambda e: e.tensor_tensor(out=mtmp[:, t0:t0 + tn], in0=pb_[:, 0:tn], in1=G[b][:, t0:t0 + tn], op=ALU.mult),
                                 [prb, R_G[b]], [R_mtmp])
                            if b == 1:
                                B.op(POOL_TT, lambda e: e.tensor_tensor(out=macc[:, t0:t0 + tn], in0=macc[:, t0:t0 + tn], in1=mtmp[:, t0:t0 + tn],
                                                                         op=ALU.add), [R_macc, R_mtmp], [R_macc])
                            else:
                                B.op(POOL_TT, lambda e: e.tensor_tensor(out=mergedT[:, fc, t0:t0 + tn], in0=macc[:, t0:t0 + tn],
                                                                         in1=mtmp[:, t0:t0 + tn], op=ALU.add), [R_macc, R_mtmp], [R_mT3])
            B.barrier()
        P03.close()

        if debug:
            B.dma(dbg_mg[:, :, :], mergedT[:], reads=[R_mT3], q="scalar")
        if stop_after <= 3:
            raise _Stop()
        P46 = ExitStack()
        acc = B.sb("acc", [128, 9, 2048], F32, P46)
        R_acc = [B.res("acc%d" % i) for i in range(9)]
        with ExitStack() as s4:
            wo = [B.sb("wo%d" % i, [128, 16, 512], BF16, s4) for i in range(2)]
            R_wo = [B.res(), B.res()]
            xr = [B.sb("xr%d" % i, [128, 512], F32, s4) for i in range(8)]
            R_xr = [B.res() for i in range(8)]
            xc = [0]
            for cg in range(4):
                w_, rw = wo[cg % 2], R_wo[cg % 2]
                for j in range(4):
                    wload(w_out_b[cg * 4 + j], w_[:, :, j * 128:(j + 1) * 128], rw)
                for bi, (c0, n) in enumerate(TB_OWN):
                    po, pro = bank([0, 1, 2, 3, 4, 5, 6, 7])
                    B.op("tensor", [(lambda e, fc=fc: e.matmul(po[0:n, :], lhsT=mergedT[:, fc, c0:c0 + n], rhs=w_[:, fc, :],
                                                               start=(fc == 0), stop=(fc == 15))) for fc in range(16)], [R_mT3, rw], [pro])
                    x_, rx = xr[xc[0] % 8], R_xr[xc[0] % 8]
                    xc[0] += 1
                    B.dma(x_[0:n, :], xa[c0:c0 + n, cg * 512:(cg + 1) * 512], writes=[rx])
                    B.op("vector", lambda e: e.tensor_tensor(out=acc[0:n, bi, cg * 512:(cg + 1) * 512], in0=po[0:n, :], in1=x_[0:n, :], op=ALU.add),
                         [pro, rx], [R_acc[bi]])
            B.barrier()
        PR3.close()
        PR6 = ExitStack()
        xnT = B.sb("xnT", [128, 16, OWN], BF16, PR6, side="right")
        R_xnT = B.res("xnT")
        with ExitStack() as s4:
            gffn = B.sb("gffn", [128, 2048], F32, s4)
            R_gffn = B.res()
            B.dma(gffn[:], vecs[:, 4096:6144], writes=[R_gffn])
            xnb = [B.sb("xnb%d" % i, [128, 2048], BF16, s4) for i in range(2)]
            R_xnb = [B.res(), B.res()]
            for bi, (c0, n) in enumerate(TB_OWN):
                xb_, rxb = xnb[bi % 2], R_xnb[bi % 2]
                rms_rows(acc[0:n, bi, :], n, 2048, gffn[0:n, :], R_gffn, xb_[0:n, :], [R_acc[bi]], [rxb])
                for half in range(2):
                    v, pr = tr_bf([xb_[0:n, (half * 8 + j) * 128:(half * 8 + j + 1) * 128] for j in range(8)], n, [rxb], [4, 5, 6, 7])
                    cp("scalar" if half == 0 else "vector", xnT[:, half * 8:half * 8 + 8, c0:c0 + n], v, [pr], [R_xnT])
                B.dma(x1d[c0:c0 + n, :], acc[0:n, bi, :], reads=[R_acc[bi]], writes=[RX1D], q="scalar", semres=R_acc[bi])
                if debug:
                    B.dma(x1_o[c0:c0 + n, :], acc[0:n, bi, :], reads=[R_acc[bi]], q="scalar")
            B.barrier()
        P46.close()

        if stop_after <= 4:
            raise _Stop()
        with ExitStack() as s5:
            utb = [B.sb("utb", [128, 16, 128], BF16, s5) for _ in range(2)]
            R_utb = [B.res(), B.res()]
            gab = [B.sb("gab", [128, OWN], BF16, s5) for _ in range(2)]
            R_gab = [B.res(), B.res()]
            AB = [4, 5, 6, 7]
            a_u = [0]
            a_prev = [None]

            def a_unit():
                u = a_u[0]
                if u >= 384:
                    if a_prev[0] is not None:
                        a_prev[0]()
                        a_prev[0] = None
                    return
                a_u[0] += 1
                c, ti = divmod(u, 3)
                t0, tn = TSL[ti]
                if u == 0:
                    wload(peer_ut[0], utb[0][:], R_utb[0], eng="gpsimd")
                if ti == 0 and c + 1 < 128:
                    wload(peer_ut[c + 1], utb[(c + 1) % 2][:], R_utb[(c + 1) % 2], eng="gpsimd")
                u_, ru = utb[c % 2], R_utb[c % 2]
                pa_, pra = bank(AB)
                B.op("tensor", [(lambda e, dc=dc: e.matmul(pa_[:, 0:tn], lhsT=u_[:, dc, :], rhs=xnT[:, dc, t0:t0 + tn],
                                                           start=(dc == 0), stop=(dc == 15))) for dc in range(16)], [ru, R_xnT], [pra])
                if a_prev[0] is not None:
                    a_prev[0]()

                def gel():
                    g_, rg = gab[c % 2], R_gab[c % 2]
                    B.op("scalar", lambda e: e.activation(out=g_[:, t0:t0 + tn], in_=pa_[:, 0:tn], func=AF.Gelu), [pra], [rg])
                    if ti == 2:
                        B.dma(gad[c, :, :], g_[:, :], reads=[rg], writes=[RGAD], q="scalar", semres=rg)
                a_prev[0] = gel

            pkb = B.sb("pkb", [128, 16, 128], BF16, s5)
            R_pkb = B.res()
            wload(pk_t[:, :, :], pkb[:], R_pkb)
            v16 = B.sb("v16", [128, 9, 16, 16], F32, s5)
            x16 = B.sb("x16", [128, 9, 16, 16], U32, s5)
            R_v16 = [B.res() for i in range(9)]
            R_x16 = [B.res() for i in range(9)]
            with ExitStack() as s5a:
                wpq = [B.sb("wpq%d" % i, [128, 16, 128], BF16, s5a) for i in range(2)]
                R_wpq = [B.res(), B.res()]
                qTs = [B.sb("qTs%d" % i, [128, OWN], BF16, s5a) for i in range(2)]
                R_qTs = [B.res(), B.res()]
                s1b = [B.sb("s1b%d" % i, [128, 128], F32, s5a) for i in range(3)]
                s2b = [B.sb("s2b%d" % i, [128, 128], F32, s5a) for i in range(3)]
                R_s1b = [B.res() for _ in range(3)]
                R_s2b = [B.res() for _ in range(3)]
                p5 = Pipe()
                sc_ = [0]
                for hp in range(16):
                    w_, rw = wpq[hp % 2], R_wpq[hp % 2]
                    wload(w_pq_b[hp], w_[:], rw)
                    q_, rq = qTs[hp % 2], R_qTs[hp % 2]
                    for ti, (t0, tn) in enumerate(TSL):
                        pq, prq = bank([0, 1])
                        B.op("tensor", [(lambda e, dc=dc: e.matmul(pq[:, 0:tn], lhsT=w_[:, dc, :], rhs=xnT[:, dc, t0:t0 + tn],
                                                                   start=(dc == 0), stop=(dc == 15))) for dc in range(16)], [rw, R_xnT], [prq])
                        cp("scalar", q_[:, t0:t0 + tn], pq[:, 0:tn], [prq], [rq])
                    for bi, (c0, n) in enumerate(TB_OWN):
                        def mk5(bi=bi, c0=c0, n=n, hp=hp, q_=q_, rq=rq):
                            a_, ra = s1b[sc_[0] % 3], R_s1b[sc_[0] % 3]
                            b_, rb = s2b[sc_[0] % 3], R_s2b[sc_[0] % 3]
                            sc_[0] += 1

                            def Sa():
                                st, sr = bank([2, 3])
                                B.op("tensor", lambda e: e.matmul(st[0:n, 0:128], lhsT=q_[:, c0:c0 + n], rhs=pkb[:, hp, :], start=True, stop=True),
                                     [rq, R_pkb], [sr])
                                cp("scalar", a_[0:n, :], st[0:n, 0:128], [sr], [ra])

                            def Sb():
                                B.op("vector", lambda e: e.max(out=v16[0:n, bi, hp, 0:8], in_=a_[0:n, :]), [ra], [R_v16[bi]])
                                B.op("vector", lambda e: e.max_index(out=x16[0:n, bi, hp, 0:8], in_max=v16[0:n, bi, hp, 0:8], in_values=a_[0:n, :]),
                                     [ra, R_v16[bi]], [R_x16[bi]])
                                B.op("vector", lambda e: e.match_replace(out=b_[0:n, :], in_to_replace=v16[0:n, bi, hp, 0:8], in_values=a_[0:n, :],
                                                                         imm_value=-1e30), [ra, R_v16[bi]], [rb])
                                B.op("vector", lambda e: e.max(out=v16[0:n, bi, hp, 8:16], in_=b_[0:n, :]), [rb], [R_v16[bi]])
                                B.op("vector", lambda e: e.max_index(out=x16[0:n, bi, hp, 8:16], in_max=v16[0:n, bi, hp, 8:16], in_values=b_[0:n, :]),
                                     [rb, R_v16[bi]], [R_x16[bi]])
                            return [Sa, Sb]
                        p5.push(mk5())
                        if (hp * 9 + bi) % 2 == 0:
                            a_unit()
                p5.flush()
                B.barrier()
            x16f = B.sb("x16f", [128, 16, 16], F32, s5)
            cand = B.sb("cand", [128, 8, 256], F32, s5)
            c16 = B.sb("c16", [128, 8, 16], F32, s5)
            p16 = B.sb("p16", [128, 8, 16], U32, s5)
            pa = B.sb("pa", [128, 2, 128], U32, s5)
            paf = B.sb("paf", [128, 2, 8, 16], F32, s5)
            eq = B.sb("eq", [128, 8, 16, 16], BF16, s5)
            trio = B.sb("trio", [128, 3, 128], F32, s5)
            ce = B.sb("ce", [128, 8, 16], F32, s5)
            Q1 = [B.sb("Q1", [128, 64, 128], BF16, s5) for _ in range(2)]
            Q2 = [B.sb("Q2", [128, 64, 128], BF16, s5) for _ in range(2)]
            R_Q1, R_Q2 = [B.res(), B.res()], [B.res(), B.res()]
            iota128b = B.sb("iota128b", [128, 128], BF16, s5)
            T3b = B.sb("T3b", [128, 3, 128], BF16, s5)
            Wsb = [B.sb("Wsb%d" % i, [128, 128, 64], BF16, s5) for i in range(2)]
            iota16 = B.sb("iota16", [128, 16], F32, s5)
            Rn = {n: B.res(n) for n in ["x16f", "cand", "c16", "p16", "pa", "paf", "eq", "trio", "ce", "T3", "Q1", "Q2", "iota16"]}
            R_Wsb = [[B.res() for _ in range(16)] for _ in range(2)]
            cp("vector", iota128b[:], iota128[:], [R["iota"]], [R["iota"]])
            cp("vector", iota16[:], iota128[:, 0:16], [R["iota"]], [Rn["iota16"]])
            wc = [0]
            for bi, (c0, n) in enumerate(TB_OWN):
                cp("vector", x16f[0:n].rearrange("p a b -> p (a b)"), x16[0:n, bi].rearrange("p a b -> p (a b)"), [R_x16[bi]], [Rn["x16f"]])
                vv = v16[0:n, bi].rearrange("p (h t) a -> p h t a", t=2)
                B.op("vector", lambda e: e.tensor_tensor(out=cand[0:n].rearrange("p h (a b) -> p h a b", a=16),
                                                         in0=vv[:, :, 0, :].unsqueeze(3).to_broadcast([n, 8, 16, 16]),
                                                         in1=vv[:, :, 1, :].unsqueeze(2).to_broadcast([n, 8, 16, 16]), op=ALU.add),
                     [R_v16[bi]], [Rn["cand"]])
                for h in range(8):
                    B.op("vector", lambda e, h=h: e.max(out=c16[0:n, h, 0:8], in_=cand[0:n, h, :]), [Rn["cand"]], [Rn["c16"]])
                    a_unit()
                    B.op("vector", lambda e, h=h: e.max_index(out=p16[0:n, h, 0:8], in_max=c16[0:n, h, 0:8], in_values=cand[0:n, h, :]),
                         [Rn["cand"], Rn["c16"]], [Rn["p16"]])
                    B.op("vector", lambda e, h=h: e.match_replace(out=cand[0:n, h, :], in_to_replace=c16[0:n, h, 0:8], in_values=cand[0:n, h, :],
                                                                  imm_value=-1e30), [Rn["cand"], Rn["c16"], Rn["p16"]], [Rn["cand"]])
                    a_unit()
                    B.op("vector", lambda e, h=h: e.max(out=c16[0:n, h, 8:16], in_=cand[0:n, h, :]), [Rn["cand"]], [Rn["c16"]])
                    B.op("vector", lambda e, h=h: e.max_index(out=p16[0:n, h, 8:16], in_max=c16[0:n, h, 8:16], in_values=cand[0:n, h, :]),
                         [Rn["cand"], Rn["c16"]], [Rn["p16"]])
                    a_unit()
                p16f = p16[0:n].rearrange("p h k -> p (h k)")
                B.op("vector", lambda e: e.tensor_single_scalar(out=pa[0:n, 0, :], in_=p16f, scalar=4, op=ALU.logical_shift_right), [Rn["p16"]], [Rn["pa"]])
                B.op("vector", lambda e: e.tensor_single_scalar(out=pa[0:n, 1, :], in_=p16f, scalar=15, op=ALU.bitwise_and), [Rn["p16"]], [Rn["pa"]])
                cp("vector", paf[0:n].rearrange("p t h k -> p (t h k)"), pa[0:n].rearrange("p t k -> p (t k)"), [Rn["pa"]], [Rn["paf"]])
                xf = x16f[0:n].rearrange("p (h t) a -> p h t a", t=2)
                for t in range(2):
                    B.op("vector", lambda e, t=t: e.tensor_tensor(out=eq[0:n], in0=iota16[0:n, :].unsqueeze(1).unsqueeze(1).to_broadcast([n, 8, 16, 16]),
                                                                  in1=paf[0:n, t].unsqueeze(3).to_broadcast([n, 8, 16, 16]), op=ALU.is_equal),
                         [Rn["iota16"], Rn["paf"]], [Rn["eq"]])
                    B.op("vector", lambda e, t=t: e.tensor_tensor(out=eq[0:n], in0=eq[0:n], in1=xf[:, :, t, :].unsqueeze(2).to_broadcast([n, 8, 16, 16]),
                                                                  op=ALU.mult), [Rn["eq"], Rn["x16f"]], [Rn["eq"]])
                    B.op("vector", lambda e, t=t: e.tensor_reduce(out=trio[0:n, t, :].rearrange("p (h k) -> p h k", h=8), in_=eq[0:n], axis=AX.X, op=ALU.add),
                         [Rn["eq"]], [Rn["trio"]])
                B.op("vector", lambda e: e.tensor_tensor(out=ce[0:n], in0=c16[0:n], in1=c16[0:n, :, 0:1].to_broadcast([n, 8, 16]), op=ALU.subtract),
                     [Rn["c16"]], [Rn["ce"]])
                B.op("scalar", lambda e: e.activation(out=ce[0:n], in_=ce[0:n], func=AF.Exp), [Rn["ce"]], [Rn["ce"]])
                B.op("vector", lambda e: e.tensor_reduce(out=ss[0:n, 0:8], in_=ce[0:n], axis=AX.X, op=ALU.add), [Rn["ce"]], [R["ss"]])
                B.op("vector", lambda e: e.reciprocal(out=ss2[0:n, 0:8], in_=ss[0:n, 0:8]), [R["ss"]], [R["ss2"]])
                B.op("vector", lambda e: e.tensor_tensor(out=trio[0:n, 2, :].rearrange("p (h k) -> p h k", h=8), in0=ce[0:n],
                                                         in1=ss2[0:n, 0:8].unsqueeze(2).to_broadcast([n, 8, 16]), op=ALU.mult),
                     [Rn["ce"], R["ss2"]], [Rn["trio"]])
                pt, pr = bank([0, 1])
                B.op("tensor", [(lambda e, j=j: e.transpose(pt[:, j * 128:j * 128 + n], trio[0:n, j, :], ident_f[0:n, 0:n])) for j in range(3)],
                     [Rn["trio"], R["ident"]], [pr])
                cp("scalar", T3b[:, :, 0:n], pt[:, 0:384].rearrange("p (j t) -> p j t", j=3)[:, :, 0:n], [pr], [Rn["T3"]])
                for hf in range((n + 63) // 64):
                    t0 = hf * 64
                    qi = wc[0] % 2
                    Q1_, Q2_, rQ1, rQ2 = Q1[qi], Q2[qi], R_Q1[qi], R_Q2[qi]
                    W_, rWs = Wsb[qi], R_Wsb[qi]
                    wc[0] += 1
                    B.op("vector", lambda e: e.tensor_tensor(out=Q1_[:], in0=iota128b[:, :].unsqueeze(1).to_broadcast([128, 64, 128]),
                                                             in1=T3b[:, 0, t0:t0 + 64].unsqueeze(2).to_broadcast([128, 64, 128]), op=ALU.is_equal),
                         [R["iota"], Rn["T3"]], [rQ1])
                    B.op("vector", lambda e: e.tensor_tensor(out=Q1_[:], in0=Q1_[:], in1=T3b[:, 2, t0:t0 + 64].unsqueeze(2).to_broadcast([128, 64, 128]),
                                                             op=ALU.mult), [rQ1, Rn["T3"]], [rQ1])
                    B.op("vector", lambda e: e.tensor_tensor(out=Q2_[:], in0=iota128b[:, :].unsqueeze(1).to_broadcast([128, 64, 128]),
                                                             in1=T3b[:, 1, t0:t0 + 64].unsqueeze(2).to_broadcast([128, 64, 128]), op=ALU.is_equal),
                         [R["iota"], Rn["T3"]], [rQ2])
                    for t4 in range(16):
                        pw, prw = bank([2, 3])
                        pwv = pw[:, :].rearrange("r (c t) -> r t c", t=4)
                        B.op("tensor", [(lambda e, tt=tt: e.matmul(pwv[:, tt, :], lhsT=Q2_[:, t4 * 4 + tt, :], rhs=Q1_[:, t4 * 4 + tt, :],
                                                                   start=True, stop=True)) for tt in range(4)], [rQ1, rQ2], [prw])
                        cp("scalar", W_[:, :, t4 * 4:t4 * 4 + 4],
                           pw[:, :].rearrange("r (c t) -> r c t", t=4), [prw], [rWs[t4]])
                        if t4 % 2 == 1:
                            a_unit()
                    B.dma(wd[:, :, c0 + t0:c0 + t0 + 64].rearrange("c r t -> r c t"), W_[:], reads=rWs, writes=[RWD], q="scalar", semres=rWs[0])
            while a_u[0] < 384 or a_prev[0] is not None:
                a_unit()
            B.barrier()
        WST.close()

        PR6.close()
        JS.close()
        if stop_after <= 5:
            raise _Stop()
        P6 = ExitStack()
        acc = B.sb("acc6", [128, 9, 2048], F32, P6)
        R_acc = [B.res("acc6_%d" % i) for i in range(9)]
        for bi, (c0, n) in enumerate(TB_OWN):
            B.dma(acc[0:n, bi, :], x1d[c0:c0 + n, :], reads=[RX1D], writes=[R_acc[bi]])
        GS = 8
        NG = 128 // GS
        with ExitStack() as s6:
            NSTG = 2
            stg = [B.sb("stg%d" % i, [128, 2048], F32, s6) for i in range(NSTG)]
            R_stg = [B.res() for _ in range(NSTG)]
            vbf = [B.sb("vbf%d" % i, [128, GS, 2048], BF16, s6) for i in range(2)]
            R_vbf = [[B.res() for ci in range(GS)] for i in range(2)]
            WA = [B.sb("WA%d" % i, [128, GS, OWN], BF16, s6) for i in range(2)]
            R_WA = [[B.res() for ci in range(GS)] for i in range(2)]
            NWC = GS
            wcb = [B.sb("wcb%d" % i, [128, OWN], BF16, s6) for i in range(NWC)]
            R_wcb = [B.res() for i in range(NWC)]
            lc = [0]

            def sload(src, dst, dres, eng):
                i = lc[0] % NSTG
                lc[0] += 1
                st = stg[i][:]
                if len(src.shape) == 3:
                    st = st.rearrange("p (a b) -> p a b", a=src.shape[1])
                B.dma(st, src, writes=[R_stg[i]])
                cp(eng, dst, st, [R_stg[i]], [dres])

            def stage_A(g):
                gi = g % 2
                B.dma(WA[gi][:, :, :], gad[g * GS:(g + 1) * GS, :, :].rearrange("c r t -> r c t"), reads=[RGAD], writes=R_WA[gi],
                      semres=R_WA[gi][0])
                mults = []
                for ci in range(GS):
                    c = g * GS + ci
                    sload(peer_v[c * 128:(c + 1) * 128, :], vbf[gi][:, ci, :], R_vbf[gi][ci], "gpsimd" if c % 4 == 0 else "scalar")
                    wc_, rwc = wcb[c % NWC], R_wcb[c % NWC]
                    B.dma(wc_[:, :], wd[c, :, 0:OWN], reads=[RWD], writes=[rwc])

                    def mult(ci=ci, wc_=wc_, rwc=rwc):
                        B.op("vector", lambda e: e.tensor_tensor(out=WA[gi][:, ci, :], in0=WA[gi][:, ci, :], in1=wc_[:, :], op=ALU.mult),
                             [rwc, R_WA[gi][ci]], [R_WA[gi][ci]])
                    mults.append(mult)
                return mults

            def stage_O(g, pending):
                gi = g % 2
                k = 0
                for bi, (c0, n) in enumerate(TB_OWN):
                    for cg in range(4):
                        po, pro = bank([0, 1, 2, 3, 4, 5, 6, 7])
                        B.op("tensor", [(lambda e, ci=ci: e.matmul(po[0:n, :], lhsT=WA[gi][:, ci, c0:c0 + n], rhs=vbf[gi][:, ci, cg * 512:(cg + 1) * 512],
                                                                   start=(ci == 0), stop=(ci == GS - 1))) for ci in range(GS)],
                             R_WA[gi] + R_vbf[gi], [pro])
                        B.op("vector", lambda e: e.tensor_tensor(out=acc[0:n, bi, cg * 512:(cg + 1) * 512], in0=po[0:n, :],
                                                                 in1=acc[0:n, bi, cg * 512:(cg + 1) * 512], op=ALU.add), [pro, R_acc[bi]], [R_acc[bi]])
                        k += 1
                        if pending and k % 4 == 0:
                            pending.pop(0)()
                while pending:
                    pending.pop(0)()

            for m_ in stage_A(0):
                m_()
            for g in range(NG):
                pend = stage_A(g + 1) if g + 1 < NG else []
                stage_O(g, pend)
            for bi, (c0, n) in enumerate(TB_OWN):
                B.dma(y_o[c0:c0 + n, :], acc[0:n, bi, :], reads=[R_acc[bi]], q="scalar")
    except _Stop:
        pass
    B.emit()
    return nc


def _prep(inp):
    f32 = np.float32
    g = lambda k: np.asarray(inp[k], dtype=f32)
    w_in = g("w_in")[0]
    shared = {
        "w_in_b": np.ascontiguousarray(w_in.reshape(16, 128, 84, 128).transpose(2, 1, 0, 3)),
        "w_br_b": np.ascontiguousarray(g("w_branch")[0].reshape(16, 128, 16, 128).transpose(2, 1, 0, 3)),
        "w_out_b": np.ascontiguousarray(g("w_out")[0].reshape(16, 128, 16, 128).transpose(2, 1, 0, 3)),
        "w_pq_b": np.ascontiguousarray(g("peer_w_query")[0].reshape(16, 128, 16, 128).transpose(2, 1, 0, 3)),
        "w_mkv_b": np.ascontiguousarray(g("w_mem_kv")[0].reshape(16, 128, 8, 128).transpose(2, 1, 0, 3)),
        "peer_ut": np.ascontiguousarray(g("peer_u")[0].reshape(128, 128, 16, 128).transpose(0, 3, 2, 1)),
        "peer_v": np.ascontiguousarray(g("peer_v")[0]),
        "pk_t": np.ascontiguousarray(g("peer_sub_keys")[0].reshape(16, 128, 128).transpose(2, 0, 1)),
        "bgT": np.ascontiguousarray(g("b_gate")[0].reshape(48, 128).T),
        "bsT": np.ascontiguousarray(g("cm_bs")[0].T),
        "wsl": np.ascontiguousarray(g("cm_ws")[0].transpose(1, 0, 2)),
    }
    vec = np.concatenate([
        g("norm_mix_g")[0], g("norm_mem_g")[0], g("norm_ffn_g")[0],
        g("da_qn_g")[0], g("da_kn_g")[0], g("da_out_g")[0], g("cm_ln_g")[0], g("cm_ln_b")[0],
        g("mem_qn_g")[0], g("mem_kn_g")[0],
        g("da_lambda_q1")[0], g("da_lambda_k1")[0], g("da_lambda_q2")[0], g("da_lambda_k2")[0]])
    assert vec.shape[0] == 6144 + NSV
    shared["vecs"] = np.ascontiguousarray(np.tile(vec[None, :], (128, 1)))
    inv = (np.float32(500000.0) ** (-(np.arange(8, dtype=f32)) / np.float32(8))).astype(f32)
    xp, xs, mp = g("x_prompt"), g("x_sample"), g("mem_prompt")
    cdk, cdv, cmk, cmv = g("cache_da_k")[0], g("cache_da_v")[0], g("cache_mem_k")[0], g("cache_mem_v")[0]
    maps = []
    for c in range(8):
        b, par = c // 2, c % 2
        own = OWN_BLKS[par]
        oth = OWN_BLKS[1 - par]
        rows_a = np.concatenate([np.arange(t * 128, (t + 1) * 128) for t in own])
        rows_b = np.concatenate([np.arange(t * 128, (t + 1) * 128) for t in oth])
        xa = np.concatenate([xp[b][rows_a], xs[c], xp[b][rows_b]], axis=0)
        pos = np.concatenate([rows_a, 1024 + np.arange(64), rows_b]).astype(f32)
        ang = pos[:, None] * inv[None, :]
        cs_flat = np.concatenate([np.cos(ang), np.sin(ang)], axis=1).astype(f32)
        cs = np.zeros((128, 17, 16), f32)
        for bi in range(17):
            if bi < 8:
                r0, n = bi * 128, 128
            elif bi == 8:
                r0, n = 1024, 64
            else:
                r0, n = 1088 + (bi - 9) * 128, 128
            cs[0:n, bi, :] = cs_flat[r0:r0 + n]
        cmask = np.zeros((128, 8, 128), f32)
        for i in range(8):
            if oth[i] < own[i]:
                cmask[:, i, :] = 1.0
        m = dict(shared)
        m.update({
            "xa": np.ascontiguousarray(xa), "cs": cs, "cmask": cmask,
            "mem_x": np.ascontiguousarray(mp[b]),
            "c_dak": np.ascontiguousarray(cdk[c].reshape(1024, 1024)),
            "c_dav": np.ascontiguousarray(cdv[c].reshape(1024, 1024)),
            "c_mk": np.ascontiguousarray(cmk[c].reshape(256, 512)),
            "c_mv": np.ascontiguousarray(cmv[c].reshape(256, 512)),
        })
        maps.append(m)
    return maps


_LAST = {}


def kernel(**inputs):
    maps = _prep(inputs)
    nc = build_program(debug=DEBUG)
    res = run_bass_kernel_spmd(nc, maps, core_ids=list(range(8)))
    outs = res.results
    f32 = np.float32
    yp = np.zeros((4, 2048, 2048), f32)
    ys = np.zeros((8, 64, 2048), f32)
    kp = np.zeros((1, 4, 2048, 8, 128), f32)
    vp = np.zeros((1, 4, 2048, 8, 128), f32)
    mkp = np.zeros((1, 4, 256, 4, 128), f32)
    mvp = np.zeros((1, 4, 256, 4, 128), f32)
    ksn = np.zeros((1, 8, 64, 8, 128), f32)
    vsn = np.zeros((1, 8, 64, 8, 128), f32)
    cvs = np.zeros((1, 8, 64, 512), f32)
    for c in range(8):
        b, par = c // 2, c % 2
        o = outs[c]
        rows_a = np.concatenate([np.arange(t * 128, (t + 1) * 128) for t in OWN_BLKS[par]])
        yp[b][rows_a] = o["y"][0:1024]
        ys[c] = o["y"][1024:1088]
        kp[0, b][rows_a] = o["ok"][0:1024].reshape(1024, 8, 128)
        vp[0, b][rows_a] = o["ov"][0:1024].reshape(1024, 8, 128)
        ksn[0, c] = o["ok"][1024:1088].reshape(64, 8, 128)
        vsn[0, c] = o["ov"][1024:1088].reshape(64, 8, 128)
        cvs[0, c] = o["ocv"]
        if par == 0:
            mkp[0, b] = o["omk"].reshape(256, 4, 128)
            mvp[0, b] = o["omv"].reshape(256, 4, 128)
        if DEBUG:
            _LAST.setdefault("x1", {})[c] = o["x1dbg"]
    return (yp, ys, kp, vp, mkp, mvp, ksn, vsn, cvs)
```
